# Optimizing a Trainium2 kernel written in Bass

```python
import math
import jax, jax.numpy as jnp
from jax import lax
import numpy as np

D_MODEL = 1024
BATCH = 16
SEQ = 256
DEPTH = 2
DEC_BATCH = 8
DEC_SEQ = 1024
PAST_LEN = 512

GRID_W = 64
N_MIXERS = 2
N_ATTN_LAYERS = (DEPTH + 1) // 2
N_RET_LAYERS = DEPTH // 2
N_HEADS = 8
N_KV_HEADS = 2
HEAD_DIM = D_MODEL // N_HEADS
GROUP = N_HEADS // N_KV_HEADS
ROPE_BASE = 10000.0
Q_BLOCK = 128
RET_HEADS = 4
RET_DK = D_MODEL // RET_HEADS
RET_DV = D_MODEL // RET_HEADS
RET_CHUNK = 128
D_FF = 4 * D_MODEL
EPS = 1e-6

kernel_name = 'hybrid_flow_gqa_retention_prefix_step'


def rms_norm(x, g):
    x32 = x.astype(jnp.float32)
    y = x32 * lax.rsqrt(jnp.mean(x32 * x32, axis=-1, keepdims=True) + EPS)
    return (y * g.astype(jnp.float32)).astype(x.dtype)


def adaln_params(cond, w, b):
    m = (jax.nn.silu(cond) @ w + b)[:, None, :]
    return jnp.split(m, 6, axis=-1)


def modulate(h, shift, scale):
    return h * (1.0 + scale) + shift


def axial_rope(x):
    n, d = x.shape[1], x.shape[-1]
    quarter = d // 4
    n_rows = n // GRID_W
    rows = jnp.broadcast_to(jnp.arange(n_rows)[:, None], (n_rows, GRID_W)).reshape(-1)
    cols = jnp.broadcast_to(jnp.arange(GRID_W)[None, :], (n_rows, GRID_W)).reshape(-1)
    freqs = ROPE_BASE ** (-jnp.arange(quarter, dtype=jnp.float32) / quarter)
    ang = jnp.stack([rows.astype(jnp.float32)[:, None] * freqs,
                     cols.astype(jnp.float32)[:, None] * freqs], axis=1)
    cos = jnp.cos(ang)[None, :, None]
    sin = jnp.sin(ang)[None, :, None]
    xs = x.astype(jnp.float32).reshape(x.shape[:-1] + (2, 2, quarter))
    x1 = xs[..., 0, :]
    x2 = xs[..., 1, :]
    out = jnp.stack([x1 * cos - x2 * sin, x2 * cos + x1 * sin], axis=-2)
    return out.reshape(x.shape).astype(x.dtype)


def attn_qkv(h, w_qkv, q_norm, k_norm, use_rope):
    b, n, _ = h.shape
    qkv = h @ w_qkv
    nq = N_HEADS * HEAD_DIM
    nk = N_KV_HEADS * HEAD_DIM
    q = qkv[..., :nq].reshape(b, n, N_HEADS, HEAD_DIM)
    k = qkv[..., nq:nq + nk].reshape(b, n, N_KV_HEADS, HEAD_DIM)
    v = qkv[..., nq + nk:].reshape(b, n, N_KV_HEADS, HEAD_DIM)
    q = rms_norm(q, q_norm)
    k = rms_norm(k, k_norm)
    if use_rope:
        q = axial_rope(q)
        k = axial_rope(k)
    return q, k, v


def block_attention(q, k, v):
    b, nq = q.shape[:2]
    nb = nq // Q_BLOCK
    qb = q.reshape(b, nb, Q_BLOCK, N_KV_HEADS, GROUP, HEAD_DIM).transpose(1, 0, 2, 3, 4, 5)
    scale = HEAD_DIM ** -0.5
    k32 = k.astype(jnp.float32)
    v32 = v.astype(jnp.float32)

    def one_block(qi):
        s = jnp.einsum('bqkgd,bskd->bkgqs', qi.astype(jnp.float32), k32) * scale
        p = jax.nn.softmax(s, axis=-1)
        return jnp.einsum('bkgqs,bskd->bqkgd', p, v32).astype(q.dtype)

    o = lax.map(one_block, qb)
    return o.transpose(1, 0, 2, 3, 4, 5).reshape(b, nq, N_HEADS * HEAD_DIM)


def retention_chunkwise(q, k, v, log_gamma, s0, strict):
    b, h, n, dk = q.shape
    dv = v.shape[-1]
    nc = n // RET_CHUNK
    idx = jnp.arange(RET_CHUNK, dtype=jnp.float32)
    diff = idx[:, None] - idx[None, :]
    mask = (diff > 0) if strict else (diff >= 0)
    lg = log_gamma.astype(jnp.float32)
    decay_mask = jnp.where(mask[None], jnp.exp(lg[:, None, None] * jnp.where(mask, diff, 0.0)[None]), 0.0)
    q_decay = jnp.exp(lg[:, None] * (idx + 1.0)[None])
    k_decay = jnp.exp(lg[:, None] * (RET_CHUNK - 1.0 - idx)[None])
    chunk_decay = jnp.exp(lg * RET_CHUNK)

    def to_chunks(x):
        return x.astype(jnp.float32).reshape(b, h, nc, RET_CHUNK, x.shape[-1]).transpose(2, 0, 1, 3, 4)

    def step(s, inp):
        qc, kc, vc = inp
        scores = jnp.einsum('bhid,bhjd->bhij', qc, kc) * decay_mask[None]
        intra = jnp.einsum('bhij,bhjv->bhiv', scores, vc)
        inter = jnp.einsum('bhid,bhdv->bhiv', qc, s) * q_decay[None, :, :, None]
        s_new = s * chunk_decay[None, :, None, None] + jnp.einsum(
            'bhjd,bhjv->bhdv', kc * k_decay[None, :, :, None], vc)
        return s_new, intra + inter

    s_fin, o = lax.scan(step, s0.astype(jnp.float32), (to_chunks(q), to_chunks(k), to_chunks(v)))
    o = o.transpose(1, 2, 0, 3, 4).reshape(b, h, n, dv)
    return o, s_fin


def retention_mixer(h, w_qkvg, decay_logit, gn_w, w_o, s0_fwd, s0_bwd, use_rope):
    b, n, _ = h.shape
    proj = h @ w_qkvg
    dk_tot = RET_HEADS * RET_DK
    dv_tot = RET_HEADS * RET_DV
    q = proj[..., :dk_tot].reshape(b, n, RET_HEADS, RET_DK)
    k = proj[..., dk_tot:2 * dk_tot].reshape(b, n, RET_HEADS, RET_DK)
    v = proj[..., 2 * dk_tot:2 * dk_tot + dv_tot].reshape(b, n, RET_HEADS, RET_DV)
    g = proj[..., 2 * dk_tot + dv_tot:]
    if use_rope:
        q = axial_rope(q)
        k = axial_rope(k)
    q = q * (RET_DK ** -0.5)
    q = q.transpose(0, 2, 1, 3)
    k = k.transpose(0, 2, 1, 3)
    v = v.transpose(0, 2, 1, 3)
    log_gamma = jax.nn.log_sigmoid(decay_logit.astype(jnp.float32))
    o_f, s_f = retention_chunkwise(q, k, v, log_gamma[0], s0_fwd, False)
    o_b, s_b = retention_chunkwise(jnp.flip(q, 2), jnp.flip(k, 2), jnp.flip(v, 2),
                                   log_gamma[1], s0_bwd, True)
    o = o_f + jnp.flip(o_b, 2)
    mu = jnp.mean(o, axis=-1, keepdims=True)
    var = jnp.mean(jnp.square(o - mu), axis=-1, keepdims=True)
    o = (o - mu) * lax.rsqrt(var + EPS)
    o = o.transpose(0, 2, 1, 3).reshape(b, n, dv_tot) * gn_w.astype(jnp.float32)
    out = (jax.nn.silu(g) * o.astype(h.dtype)) @ w_o
    return out, s_f, s_b


def sq_relu_mlp(h, w1, w2):
    return jnp.square(jax.nn.relu(h @ w1)) @ w2


def setup_inputs(seed: int = 0) -> dict:
    key = jax.random.key(seed)
    ks = jax.random.split(key, 24)
    f32 = jnp.float32
    D = D_MODEL
    qkv_w = (N_HEADS + 2 * N_KV_HEADS) * HEAD_DIM
    ret_w = 2 * RET_HEADS * RET_DK + 2 * RET_HEADS * RET_DV
    gamma = 1.0 - 2.0 ** (-5.0 - jnp.arange(RET_HEADS, dtype=f32))
    base_logit = jnp.log(gamma) - jnp.log1p(-gamma)
    return {
        'x_prompt': jax.random.normal(ks[0], (BATCH, SEQ, D), f32),
        'x_sample': jax.random.normal(ks[1], (DEC_BATCH, DEC_SEQ, D), f32),
        'cache_k': jax.random.normal(ks[2], (DEC_BATCH, N_ATTN_LAYERS, PAST_LEN, N_KV_HEADS, HEAD_DIM), f32),
        'cache_v': jax.random.normal(ks[3], (DEC_BATCH, N_ATTN_LAYERS, PAST_LEN, N_KV_HEADS, HEAD_DIM), f32),
        'state_ret': 2.0 * jax.random.normal(ks[4], (DEC_BATCH, N_RET_LAYERS, 2, RET_HEADS, RET_DK, RET_DV), f32),
        'c': jax.random.normal(ks[5], (DEC_BATCH, D), f32),
        'c_ctx': jax.random.normal(ks[6], (D,), f32),
        'w_mod': 0.5 * D ** -0.5 * jax.random.normal(ks[7], (DEPTH, D, 6 * D), f32),
        'b_mod': 0.02 * jax.random.normal(ks[8], (DEPTH, 6 * D), f32),
        'norm_g': 1.0 + 0.02 * jax.random.normal(ks[9], (DEPTH, 2, D), f32),
        'attn_w_qkv': D ** -0.5 * jax.random.normal(ks[10], (N_ATTN_LAYERS, D, qkv_w), f32),
        'attn_q_norm': 1.0 + 0.02 * jax.random.normal(ks[11], (N_ATTN_LAYERS, HEAD_DIM), f32),
        'attn_k_norm': 1.0 + 0.02 * jax.random.normal(ks[12], (N_ATTN_LAYERS, HEAD_DIM), f32),
        'attn_w_o': D ** -0.5 * jax.random.normal(ks[13], (N_ATTN_LAYERS, N_HEADS * HEAD_DIM, D), f32),
        'ret_w_qkvg': D ** -0.5 * jax.random.normal(ks[14], (N_RET_LAYERS, D, ret_w), f32),
        'ret_decay_logit': base_logit[None, None, :] + 0.1 * jax.random.normal(ks[15], (N_RET_LAYERS, 2, RET_HEADS), f32),
        'ret_gn_w': 1.0 + 0.02 * jax.random.normal(ks[16], (N_RET_LAYERS, RET_HEADS * RET_DV), f32),
        'ret_w_o': (RET_HEADS * RET_DV) ** -0.5 * jax.random.normal(ks[17], (N_RET_LAYERS, RET_HEADS * RET_DV, D), f32),
        'mlp_w1': D ** -0.5 * jax.random.normal(ks[18], (DEPTH, D, D_FF), f32),
        'mlp_w2': D_FF ** -0.5 * jax.random.normal(ks[19], (DEPTH, D_FF, D), f32),
        'final_norm_g': 1.0 + 0.02 * jax.random.normal(ks[20], (D,), f32),
    }


def reference(x_prompt, x_sample, cache_k, cache_v, state_ret, c, c_ctx,
              w_mod, b_mod, norm_g, attn_w_qkv, attn_q_norm, attn_k_norm, attn_w_o,
              ret_w_qkvg, ret_decay_logit, ret_gn_w, ret_w_o, mlp_w1, mlp_w2, final_norm_g):
    xp = x_prompt
    xs = x_sample
    new_k, new_v, new_s = [], [], []
    for i in range(DEPTH):
        mc = adaln_params(c_ctx[None, :], w_mod[i], b_mod[i])
        ms = adaln_params(c, w_mod[i], b_mod[i])
        hp = modulate(rms_norm(xp, norm_g[i, 0]), mc[0], mc[1])
        hs = modulate(rms_norm(xs, norm_g[i, 0]), ms[0], ms[1])
        j = i // N_MIXERS
        if i % N_MIXERS == 0:
            qp, kp, vp = attn_qkv(hp, attn_w_qkv[j], attn_q_norm[j], attn_k_norm[j], False)
            op = block_attention(qp, kp, vp) @ attn_w_o[j]
            new_k.append(kp)
            new_v.append(vp)
            qs, ks_, vs = attn_qkv(hs, attn_w_qkv[j], attn_q_norm[j], attn_k_norm[j], True)
            k_all = jnp.concatenate([cache_k[:, j].astype(ks_.dtype), ks_], axis=1)
            v_all = jnp.concatenate([cache_v[:, j].astype(vs.dtype), vs], axis=1)
            os_ = block_attention(qs, k_all, v_all) @ attn_w_o[j]
        else:
            zero = jnp.zeros((xp.shape[0], RET_HEADS, RET_DK, RET_DV), jnp.float32)
            op, sf, sb = retention_mixer(hp, ret_w_qkvg[j], ret_decay_logit[j], ret_gn_w[j], ret_w_o[j],
                                         zero, zero, False)
            new_s.append(jnp.stack([sf, sb], axis=1))
            os_, _, _ = retention_mixer(hs, ret_w_qkvg[j], ret_decay_logit[j], ret_gn_w[j], ret_w_o[j],
                                        state_ret[:, j, 0], state_ret[:, j, 1], True)
        xp = xp + mc[2] * op
        xs = xs + ms[2] * os_
        hp = modulate(rms_norm(xp, norm_g[i, 1]), mc[3], mc[4])
        hs = modulate(rms_norm(xs, norm_g[i, 1]), ms[3], ms[4])
        xp = xp + mc[5] * sq_relu_mlp(hp, mlp_w1[i], mlp_w2[i])
        xs = xs + ms[5] * sq_relu_mlp(hs, mlp_w1[i], mlp_w2[i])
    y_prompt = rms_norm(xp, final_norm_g)
    y_sample = rms_norm(xs, final_norm_g)
    new_cache_k = jnp.stack(new_k, axis=1)
    new_cache_v = jnp.stack(new_v, axis=1)
    new_state_ret = jnp.stack(new_s, axis=1)
    return (y_prompt, y_sample, new_cache_k, new_cache_v, new_state_ret)
```

```python
import numpy as np
import concourse.bass as bass
import concourse.mybir as mybir
from concourse.bass_utils import run_bass_kernel_spmd

F32 = mybir.dt.float32
BF16 = mybir.dt.bfloat16
I32 = mybir.dt.int32
AF = mybir.ActivationFunctionType
ALU = mybir.AluOpType

NCORES = 8
D = 1024
NT = 1536
EPS = 1e-6
NSLOT = 5
SAME_ENGINE_SYNC = True


class Buf:
    __slots__ = ("name", "w", "r", "sem", "nd", "excl")

    def __init__(self, name, sem=None):
        self.name = name
        self.excl = False
        self.w = None
        self.r = []
        self.sem = sem
        self.nd = 0


class Eng:
    def __init__(self, name, eng, sem, is_pe=False):
        self.name = name
        self.eng = eng
        self.sem = sem
        self.count = 0
        self.seen = {}
        self.is_pe = is_pe
        self.pending = False


class Sched:
    def __init__(self, nc):
        self.nc = nc
        self.sems = {}
        self.engs = {}
        self.dma_toks = []
        self.other_toks = []

    def add_engine(self, name, eng, is_pe=False):
        sem = self.nc.alloc_semaphore(name=f"sem_{name}")
        e = Eng(name, eng, sem, is_pe)
        self.engs[name] = e
        self.sems[name] = sem
        return e

    def dma_buf(self, name):
        sem = self.nc.alloc_semaphore(name=f"dsem_{name}")
        self.sems["d:" + name] = sem
        return Buf(name, sem="d:" + name)

    def _wait(self, e, deps):
        best = {}
        for (k, v) in deps:
            if k == e.name and (e.is_pe or not SAME_ENGINE_SYNC):
                continue
            if best.get(k, 0) < v:
                best[k] = v
        for k, v in best.items():
            if e.seen.get(k, 0) < v:
                e.eng.wait_ge(self.sems[k], v)
                e.seen[k] = v

    @staticmethod
    def _deps(reads, writes):
        deps = []
        for b in reads:
            if b.w is not None:
                deps.append(b.w)
            if b.excl:
                deps.extend(b.r)
        for b in writes:
            if b.w is not None:
                deps.append(b.w)
            deps.extend(b.r)
        return deps

    def op(self, ename, fn, reads=(), writes=(), inc=True):
        e = self.engs[ename]
        self._wait(e, self._deps(reads, writes))
        ins = fn()
        tok = (e.name, e.count + 1)
        if inc:
            ins.then_inc(e.sem, 1)
            e.count += 1
            e.pending = False
        else:
            e.pending = True
        for b in writes:
            b.w = tok
            b.r = []
        for b in reads:
            if not b.r or b.r[-1] != tok:
                b.r.append(tok)
        return ins

    def dma(self, qname, out, in_, reads=(), writes=(), chan=None, arena=True):
        e = self.engs[qname]
        self._wait(e, self._deps(reads, writes))
        cb = chan if chan is not None else (list(writes) + list(reads))[0]
        assert cb.sem is not None, cb.name
        ins = e.eng.dma_start(out=out, in_=in_)
        ins.then_inc(self.sems[cb.sem], 16)
        cb.nd += 1
        tok = (cb.sem, 16 * cb.nd)
        for b in writes:
            b.w = tok
            b.r = []
        for b in reads:
            b.r.append(tok)
        (self.dma_toks if arena else self.other_toks).append(tok)
        return ins

    def wait_tokens(self, ename, toks):
        self._wait(self.engs[ename], toks)

    def snapshot(self):
        toks = list(self.dma_toks)
        for e in self.engs.values():
            assert not e.pending, e.name
            if e.count > 0:
                toks.append((e.name, e.count))
        return toks

    def wait_snapshot(self, toks):
        for e in self.engs.values():
            self._wait(e, toks)

    def full_barrier(self):
        toks = list(self.dma_toks) + list(self.other_toks)
        for e in self.engs.values():
            assert not e.pending, e.name
            if e.count > 0:
                toks.append((e.name, e.count))
        for e in self.engs.values():
            self._wait(e, toks)


class Ring:
    def __init__(self, items):
        self.items = list(items)
        self.i = 0

    def next(self):
        x = self.items[self.i % len(self.items)]
        self.i += 1
        return x


class Blk:
    def __init__(self, parts):
        self.parts = parts
        self.slot = None
        self.buf = None
        self.idx = None


def run_pipeline(items, offsets):
    out = []
    n = len(items)
    if n == 0:
        return out
    maxo = max(offsets)
    for s in range(n + maxo):
        for k, off in enumerate(offsets):
            i = s - off
            if 0 <= i < n and items[i][k] is not None:
                out.append(items[i][k])
    return out


def pipeline_steps(items, offsets):
    out = []
    n = len(items)
    if n == 0:
        return out
    for s_ in range(n + max(offsets)):
        cur = []
        for k, off in enumerate(offsets):
            i = s_ - off
            if 0 <= i < n and items[i][k] is not None:
                cur.append(items[i][k])
        out.append(cur)
    return out


def build(nsteps=None, dbg=False):
    nc = bass.Bass("TRN2", target_bir_lowering=False)
    S = Sched(nc)
    S.add_engine("pe", nc.tensor, True)
    S.add_engine("act", nc.scalar)
    S.add_engine("dve", nc.vector)
    S.add_engine("pool", nc.gpsimd)
    S.add_engine("sp", nc.sync)

    def din(name, shape):
        return nc.dram_tensor(name, shape, F32, kind="ExternalInput").ap()

    def dout(name, shape):
        return nc.dram_tensor(name, shape, F32, kind="ExternalOutput").ap()

    d_xs = din("xs", [1024, D]); d_xp = din("xp", [512, D])
    d_ck = din("ck", [512, 256]); d_cv = din("cv", [512, 256])
    d_sr = din("sr", [2, 4, 256, 256])
    d_cT = din("cT", [128, 16])
    d_wmod = din("w_mod", [2, D, 6 * D]); d_bmod = din("bmod", [128, 96])
    d_ng = din("ng", [128, 32]); d_fng = din("fng", [128, 8])
    d_wqkv = din("wqkv", [D, 1536]); d_qkn = din("qkn", [128, 2]); d_wo = din("wo", [D, D])
    d_wr = din("wr", [D, 4096]); d_dlog = din("dlog", [128, 8]); d_gnw = din("gnw", [128, 8])
    d_wro = din("wro", [D, D])
    d_w1 = din("w1", [2, D, 4096]); d_w2 = din("w2", [2, 4096, D])
    d_cosA = din("cosA", [128, 1024]); d_sinA = din("sinA", [128, 1024]); d_PT = din("PTm", [128, 128])
    d_rtab = din("rtab", [128, 160]); d_PT2 = din("PT2m", [128, 128]); d_rconst = din("rconst", [128, 641])
    o_yp = dout("yp", [512, D]); o_ys = dout("ys", [1024, D])
    o_nk = dout("nk", [512, 256]); o_nv = dout("nv", [512, 256])
    o_ns = dout("ns", [2, 2, 4, 256, 256])

    xT = nc.alloc_sbuf_tensor("xT", [128, 8, NT], F32)
    hT = nc.alloc_sbuf_tensor("hT", [128, 8, NT], BF16)
    wsl = nc.alloc_sbuf_tensor("wsl", [128, NSLOT, 4096], BF16)
    sqr = nc.alloc_sbuf_tensor("sqr", [128, 4, 512], BF16)
    rsr = nc.alloc_sbuf_tensor("rsr", [128, 4, 512], F32)
    tmr = nc.alloc_sbuf_tensor("tmr", [128, 3, 512], F32)
    identf = nc.alloc_sbuf_tensor("identf", [128, 128], F32)
    identb = nc.alloc_sbuf_tensor("identb", [128, 128], BF16)
    onesb = nc.alloc_sbuf_tensor("onesb", [128, 128], BF16)
    PTb = nc.alloc_sbuf_tensor("PTb", [128, 128], BF16)
    PT2b = nc.alloc_sbuf_tensor("PT2b", [128, 128], BF16)
    rtab = nc.alloc_sbuf_tensor("rtab_sb", [128, 160], F32)
    DFrow = nc.alloc_sbuf_tensor("DFrow", [128, 4, 128], F32)
    DBrow = nc.alloc_sbuf_tensor("DBrow", [128, 4, 128], F32)
    modT = nc.alloc_sbuf_tensor("modT", [128, 2, 48, 2], F32)
    bmodt = nc.alloc_sbuf_tensor("bmodt", [128, 2, 48], F32)
    ngt = nc.alloc_sbuf_tensor("ngt", [128, 2, 2, 8], F32)
    fngt = nc.alloc_sbuf_tensor("fngt", [128, 8], F32)
    gsT = nc.alloc_sbuf_tensor("gsT", [128, 2, 2, 8, 2], F32)
    cTt = nc.alloc_sbuf_tensor("cTt", [128, 8, 2], F32)
    scT = nc.alloc_sbuf_tensor("scT", [128, 8, 2], BF16)
    qknt = nc.alloc_sbuf_tensor("qknt", [128, 2], F32)
    dlt = nc.alloc_sbuf_tensor("dlt", [128, 8], F32)
    Lt = nc.alloc_sbuf_tensor("Lt", [128, 8], F32)
    nLt = nc.alloc_sbuf_tensor("nLt", [128, 8], F32)
    gnwt = nc.alloc_sbuf_tensor("gnwt", [128, 8], F32)
    pidx_i = nc.alloc_sbuf_tensor("pidx_i", [128, 4], I32)
    pidx = nc.alloc_sbuf_tensor("pidx", [128, 4], F32)
    dq = nc.alloc_sbuf_tensor("dq", [128, 8], F32)
    dk = nc.alloc_sbuf_tensor("dk", [128, 8], F32)
    cdt = nc.alloc_sbuf_tensor("cdt", [128, 8], F32)
    maskt = nc.alloc_sbuf_tensor("maskt", [128, 4, 128], F32)
    smallt = nc.alloc_sbuf_tensor("smallt", [128, 4, 16], F32)
    ARENA_BYTES = 66 * 1024
    arena = nc.alloc_sbuf_tensor("arena", [128, ARENA_BYTES // 2], BF16)
    ps = nc.alloc_psum_tensor("ps", [128, 8, 512], F32)

    class Arena:
        def __init__(self):
            self.off = 0

        def reset(self):
            self.off = 0

        def bf(self, n):
            o = self.off
            self.off += ((2 * n + 31) // 32) * 32
            assert self.off <= ARENA_BYTES, self.off
            return arena[:, o // 2: o // 2 + n]

        def f32(self, n):
            o = self.off
            self.off += ((4 * n + 31) // 32) * 32
            assert self.off <= ARENA_BYTES, self.off
            return arena[:, o // 2: o // 2 + 2 * n].bitcast(F32)

    AR = Arena()

    b_xT = [[Buf(f"xT{k}_{t}") for t in range(3)] for k in range(8)]
    b_hT = [[Buf(f"hT{k}_{t}") for t in range(3)] for k in range(8)]
    b_ps = [Buf(f"ps{i}") for i in range(8)]
    for b in b_ps:
        b.excl = True
    b_sq = [Buf(f"sq{i}") for i in range(4)]
    b_rs = [Buf(f"rs{i}") for i in range(4)]
    b_tm = [Buf(f"tm{i}") for i in range(3)]
    b_const = S.dma_buf("const")
    b_PT = S.dma_buf("ptb")
    b_mod = [[Buf(f"mod{l}_{w}") for w in range(6)] for l in range(2)]
    b_gs = [[Buf(f"gs{l}_{w}") for w in range(2)] for l in range(2)]
    b_ret = Buf("rettab")
    b_small = [Buf(f"small{i}") for i in range(4)]
    sq_ring = Ring(range(4)); rs_ring = Ring([0]); nrs_ring = Ring([1, 2, 3]); tm_ring = Ring(range(3)); small_ring = Ring(range(4))
    bank_ring = Ring(range(8))
    slot_bufs = [S.dma_buf(f"slot{i}") for i in range(NSLOT)]

    def mm(out, lhsT, rhs, start, stop, r, w, inc=True):
        return S.op("pe", lambda: nc.tensor.matmul(out, lhsT, rhs, start=start, stop=stop), r, w, inc)

    def tr(out, in_, ident, r, w, inc=True):
        return S.op("pe", lambda: nc.tensor.transpose(out, in_, ident), r, w, inc)

    def act(out, in_, func, r, w, bias=None, scale=None):
        kw = {}
        if bias is not None:
            kw["bias"] = bias
        if scale is not None:
            kw["scale"] = scale
        return S.op("act", lambda: nc.scalar.activation(out=out, in_=in_, func=func, **kw), r, w)

    def tt(out, in0, in1, op, r, w, eng="dve"):
        e = nc.vector if eng == "dve" else nc.gpsimd
        return S.op(eng, lambda: e.tensor_tensor(out=out, in0=in0, in1=in1, op=op), r, w)

    def ts(out, in0, s1, s2, op0, op1, r, w, eng="dve"):
        e = nc.vector if eng == "dve" else nc.gpsimd
        if op1 is None:
            return S.op(eng, lambda: e.tensor_scalar(out=out, in0=in0, scalar1=s1, scalar2=None, op0=op0), r, w)
        return S.op(eng, lambda: e.tensor_scalar(out=out, in0=in0, scalar1=s1, scalar2=s2, op0=op0, op1=op1), r, w)

    def stt(out, in0, scalar, in1, op0, op1, r, w):
        return S.op("dve", lambda: nc.vector.scalar_tensor_tensor(out=out, in0=in0, scalar=scalar, in1=in1, op0=op0, op1=op1), r, w)

    def cp(eng, out, in_, r, w):
        if eng == "act":
            return act(out, in_, AF.Copy, r, w)
        e = nc.vector if eng == "dve" else nc.gpsimd
        return S.op(eng, lambda: e.tensor_copy(out, in_), r, w)

    ev_flip = [0]

    def evac(out, in_, r, w):
        ev_flip[0] ^= 1
        return cp("act" if ev_flip[0] else "dve", out, in_, r, w)

    stream = []
    stream_pos = [0]

    def mkblk(parts):
        b = Blk(parts)
        b.idx = len(stream)
        stream.append(b)
        return b

    done_ptr = [0]

    def prefetch():
        lim = min(len(stream), done_ptr[0] + NSLOT)
        while stream_pos[0] < lim:
            b = stream[stream_pos[0]]
            b.slot = stream_pos[0] % NSLOT
            b.buf = slot_bufs[b.slot]
            flat = wsl[:, b.slot, :]
            for (vf, src) in b.parts:
                S.dma("pool", vf(flat), src, writes=[b.buf], arena=False)
            stream_pos[0] += 1

    def ensure(blk):
        prefetch()
        assert blk.slot is not None, (blk.idx, done_ptr[0])

    def release(blk):
        blk.done = True
        while done_ptr[0] < len(stream) and getattr(stream[done_ptr[0]], "done", False):
            done_ptr[0] += 1
        prefetch()

    def v8(flat):
        return flat.rearrange("p (k n) -> p k n", k=8)

    def wblk_cols(src2d, c0, ncols=512):
        return mkblk([(lambda f, n=ncols: v8(f)[:, :, 0:n], src2d[:, c0:c0 + ncols].rearrange("(k p) n -> p k n", p=128))])

    b_id = Buf("ident")

    def setup():
        S.op("pool", lambda: nc.gpsimd.memset(identf[:], 1.0), [], [b_id])
        S.op("pool", lambda: nc.gpsimd.affine_select(out=identf[:], in_=identf[:], pattern=[[-1, 128]], compare_op=ALU.is_equal,
                                                     fill=0.0, base=0, channel_multiplier=1), [b_id], [b_id])
        S.op("pool", lambda: nc.gpsimd.memset(onesb[:], 1.0), [], [b_id])
        S.op("pool", lambda: nc.gpsimd.memset(epsb[:], EPS), [], [b_id])
        cp("dve", identb[:], identf[:], [b_id], [b_id])

    def setup_consts():
        for (dst, src) in [(cTt[:].rearrange("p k n -> p (k n)"), d_cT), (bmodt[:].rearrange("p l c -> p (l c)"), d_bmod),
                           (ngt[:].rearrange("p a b c -> p (a b c)"), d_ng), (fngt[:], d_fng), (qknt[:], d_qkn),
                           (dlt[:], d_dlog), (gnwt[:], d_gnw)]:
            S.dma("sp", dst, src[:, :], writes=[b_const])
        S.dma("sp", rtab[:], d_rtab[:, :], writes=[b_const])

    def setup_silu():
        act(scT[:], cTt[:], AF.Silu, [b_const], [b_const])

    def setup_pt():
        S.dma("pool", PTb[:], d_PT[:, :], writes=[b_PT], arena=False)
        S.dma("pool", PT2b[:], d_PT2[:, :], writes=[b_PT], arena=False)

    b_L = Buf("Ltab")
    b_rc = S.dma_buf("rconst")
    b_e = [Buf(f"escr{h}") for h in range(4)]

    def setup_ret_pieces():
        o = 41984
        f = lambda off, n: arena[:, (o + off) // 2:(o + off) // 2 + 2 * n].bitcast(F32)
        rcv = f(0, 641)
        Dm = rcv[:, 0:128]; tri_ge = rcv[:, 128:256]; tri_gt = rcv[:, 256:384]; rowi = rcv[:, 384:512]; rowj = rcv[:, 512:640]; pcol = rcv[:, 640:641]

        def pA():
            S.dma("sp", rcv, d_rconst[:, :], writes=[b_rc])
            act(Lt[:], dlt[:], AF.Exp, [b_const], [b_L], scale=-1.0)
            ts(Lt[:], Lt[:], 1.0, None, ALU.add, None, [b_L], [b_L])
            act(Lt[:], Lt[:], AF.Ln, [b_L], [b_L])
            ts(nLt[:], Lt[:], -1.0, None, ALU.mult, None, [b_L], [b_L])
            ts(pidx[:, 3:4], pcol, -127.0, None, ALU.add, None, [b_rc], [b_L])
            ts(pidx[:, 1:2], pcol, -1.0, None, ALU.mult, None, [b_rc], [b_L])
            act(dk[:, 0:4], Lt[:, 0:4], AF.Exp, [b_L], [b_ret], scale=pidx[:, 3:4])
            act(dk[:, 4:8], Lt[:, 4:8], AF.Exp, [b_L], [b_ret], scale=pidx[:, 1:2])
            act(cdt[:], Lt[:], AF.Exp, [b_L], [b_ret], scale=-128.0)

        def pR():
            for h in range(4):
                act(DFrow[:, h, :], rowi, AF.Exp, [b_L, b_rc], [b_ret], scale=nLt[:, h:h + 1])
                act(DBrow[:, h, :], rowj, AF.Exp, [b_L, b_rc], [b_ret], scale=nLt[:, 4 + h:5 + h])
            ts(DFrow[:], DFrow[:], 0.0625, None, ALU.mult, None, [b_ret], [b_ret])
            ts(DBrow[:], DBrow[:], 0.0625, None, ALU.mult, None, [b_ret], [b_ret])

        def pM(h):
            def g():
                e = f(2624 + h * 1536, 384).rearrange("p (a n) -> p a n", a=3)
                act(e[:, 0, :], Dm, AF.Exp, [b_L, b_rc], [b_e[h]], scale=nLt[:, h:h + 1])
                act(e[:, 1, :], Dm, AF.Exp, [b_L, b_rc], [b_e[h]], scale=Lt[:, 4 + h:5 + h])
                act(e[:, 2, :], rowi, AF.Exp, [b_L, b_rc], [b_e[h]], scale=Lt[:, h:h + 1])
                tt(e[:, 0, :], e[:, 0, :], tri_ge, ALU.mult, [b_e[h], b_rc], [b_e[h]])
                tt(e[:, 1, :], e[:, 1, :], tri_gt, ALU.mult, [b_e[h], b_rc], [b_e[h]])
                tt(e[:, 0, :], e[:, 0, :], e[:, 1, :], ALU.add, [b_e[h]], [b_e[h]])
                tt(maskt[:, h, :], e[:, 0, :], e[:, 2, :], ALU.mult, [b_e[h]], [b_ret])
            return g
        return [pA, pR, pM(0), pM(1), pM(2), pM(3)]

    LI = {}

    def li_alloc():
        AR.reset()
        LI["xin"] = AR.f32(2 * 4 * 1024).rearrange("p (r j d) -> p r j d", r=2, j=4)
        LI["b"] = [S.dma_buf(f"xin{i}") for i in range(2)]

    def li_dma(t):
        def f():
            src = d_xs[t * 512:(t + 1) * 512, :] if t < 2 else d_xp[:, :]
            r = t % 2
            S.dma("sp", LI["xin"][:, r], src.rearrange("(j p) d -> p j d", p=128), writes=[LI["b"][r]])
        return f

    def li_tr(t):
        def f():
            r = t % 2
            xin = LI["xin"]
            for kc in range(8):
                bk = bank_ring.next()
                for j in range(4):
                    tr(ps[:, bk, j * 128:(j + 1) * 128], xin[:, r, j, kc * 128:(kc + 1) * 128], identf[:],
                       [LI["b"][r], b_id], [b_ps[bk]], inc=(j == 3))
                evac(xT[:, kc, t * 512:(t + 1) * 512], ps[:, bk, :], [b_ps[bk]], [b_xT[kc][t]])
        return f

    tiny_q = []

    def drain():
        while tiny_q:
            tiny_q.pop(0)()

    def mod_blk(l, hb, bank=None, spread=False):
        blk = wblk_cols(d_wmod[l], hb * 512)

        def emit():
            ensure(blk)
            sv = v8(wsl[:, blk.slot, :])
            bk = 7 if spread else (bank_ring.next() if bank is None else bank)
            ops = []
            for oc in range(4):
                for kc in range(8):
                    ops.append(lambda oc=oc, kc=kc: mm(ps[:, bk, oc * 2:oc * 2 + 2], sv[:, kc, oc * 128:(oc + 1) * 128], scT[:, kc, :],
                                                       kc == 0, kc == 7, [blk.buf, b_const], [b_ps[bk]], inc=(kc == 7)))

            def fin():
                w = hb // 2
                c0 = hb * 4
                pv = ps[:, bk, 0:8].rearrange("p (c n) -> p c n", n=2)
                for n in range(2):
                    tt(modT[:, l, c0:c0 + 4, n], pv[:, :, n], bmodt[:, l, c0:c0 + 4], ALU.add, [b_ps[bk], b_const], [b_mod[l][w]])
                release(blk)
                if hb in (3, 9):
                    wn = 0 if hb == 3 else 1
                    wsc = 1 if hb == 3 else 4
                    for n in range(2):
                        stt(gsT[:, l, wn, :, n], modT[:, l, wsc * 8:wsc * 8 + 8, n], 1.0, ngt[:, l, wn, :], ALU.add, ALU.mult,
                            [b_mod[l][wsc], b_const], [b_gs[l][wn]])
            if spread:
                tiny_q.extend(ops)
                tiny_q.append(fin)
            else:
                for o_ in ops:
                    o_()
                fin()
        return emit

    def norm_items(l, wn):
        wsh = 0 if wn == 0 else 3
        items = []
        for t in range(3):
            n = 1 if t < 2 else 0
            sl = slice(t * 512, (t + 1) * 512)
            st = {}

            def s0(t=t, sl=sl, st=st):
                st["sq"] = []
                for kc in range(8):
                    u = sq_ring.next()
                    st["sq"].append(u)
                    if kc % 2 == 0:
                        act(sqr[:, u, :], xT[:, kc, sl], AF.Square, [b_xT[kc][t]], [b_sq[u]])
                    else:
                        tt(sqr[:, u, :], xT[:, kc, sl], xT[:, kc, sl], ALU.mult, [b_xT[kc][t]], [b_sq[u]])
                    if kc == 0:
                        st["bk"] = bank_ring.next()
                    bk = st["bk"]
                    mm(ps[:, bk, :], onesb[:], sqr[:, u, :], kc == 0, kc == 7, [b_sq[u], b_id], [b_ps[bk]], inc=True)
                bk = st["bk"]
                ru = nrs_ring.next()
                st["ru"] = ru
                act(rsr[:, ru, :], ps[:, bk, :], AF.Ln, [b_ps[bk], b_id], [b_rs[ru]], bias=epsb[:, 0:1], scale=1.0 / D)
                act(rsr[:, ru, :], rsr[:, ru, :], AF.Exp, [b_rs[ru]], [b_rs[ru]], scale=-0.5)

            def s1(t=t, sl=sl, st=st, n=n):
                ru = st["ru"]
                for kc in range(8):
                    tu = tm_ring.next()
                    tt(tmr[:, tu, :], xT[:, kc, sl], rsr[:, ru, :], ALU.mult, [b_xT[kc][t], b_rs[ru]], [b_tm[tu]])
                    act(hT[:, kc, sl], tmr[:, tu, :], AF.Identity, [b_tm[tu], b_gs[l][wn], b_mod[l][wsh]], [b_hT[kc][t]],
                        bias=modT[:, l, wsh * 8 + kc, n:n + 1], scale=gsT[:, l, wn, kc, n:n + 1])
            items.append([s0, s1])
        return items

    def norm_phase(l, wn):
        return run_pipeline(norm_items(l, wn), [0, 1])

    def norm_hooks(l, wn, order=(0, 1, 2)):
        it = norm_items(l, wn)
        o0, o1, o2 = order

        def after_tile(t):
            if t == o0:
                it[o0][0]()
            elif t == o1:
                it[o1][0](); it[o0][1]()
            elif t == o2:
                it[o2][0](); it[o1][1]()

        def tail():
            it[o2][1]()
        return after_tile, tail

    def proj_ws(blk, nchunk, kcn, slot_view, rhs_fn, rhs_bufs_fn, evac_fn, tiles=(0, 1, 2), after_tile=None):
        ensure(blk)
        sv = slot_view(wsl[:, blk.slot, :])

        def grp(oc, t):
            bk = bank_ring.next()
            for kc in range(kcn):
                mm(ps[:, bk, :], sv[:, kc, oc * 128:(oc + 1) * 128], rhs_fn(kc, t), kc == 0, kc == kcn - 1,
                   [blk.buf] + rhs_bufs_fn(kc, t), [b_ps[bk]], inc=(kc == kcn - 1))
                if tiny_q and kc < kcn - 1:
                    tiny_q.pop(0)()
            evac_fn(oc, t, bk)
        if after_tile is None:
            for oc in range(nchunk):
                for t in tiles:
                    grp(oc, t)
        else:
            pending = None
            for t in tiles:
                for oc in range(nchunk):
                    grp(oc, t)
                    if pending is not None and oc == 1:
                        after_tile(pending)
                        pending = None
                pending = t
            after_tile(pending)
        release(blk)

    def tsl(t):
        return slice(t * 512, (t + 1) * 512)

    A = {}

    def attn_alloc():
        AR.reset()
        A["qT"] = AR.bf(8 * NT).rearrange("p (h n) -> p h n", h=8)
        A["kT"] = AR.bf(2 * 2048).rearrange("p (h n) -> p h n", h=2)
        A["V"] = AR.bf(16 * 256).rearrange("p (c n) -> p c n", c=16)
        A["P"] = AR.bf(8 * 512)
        A["cos"] = AR.f32(1024); A["sin"] = AR.f32(1024)
        A["qn"] = AR.bf(2 * 512).rearrange("p (r n) -> p r n", r=2)
        A["scr"] = AR.f32(2048)
        A["b_qT"] = [[Buf(f"qT{h}_{t}") for t in range(3)] for h in range(8)]
        A["b_kT"] = [[Buf(f"kT{h}_{r}") for r in range(4)] for h in range(2)]
        A["b_V"] = [Buf(f"V{r}") for r in range(4)]
        A["b_P"] = [Buf(f"P{u}") for u in range(8)]
        A["b_tab"] = S.dma_buf("atab")
        A["b_qn"] = [Buf(f"qn{i}") for i in range(2)]
        A["b_scr"] = [S.dma_buf(f"scr{i}") for i in range(4)]
        A["b_ckb"] = S.dma_buf("ckb")
        A["b_cvb"] = S.dma_buf("cvb")
        A["P_ring"] = Ring(range(8)); A["qn_ring"] = Ring(range(2)); A["t1_ring"] = Ring([0, 1])
        A["main_ring"] = Ring([0, 1, 2, 3]); A["aux_ring"] = Ring([4, 5, 6, 7])
        rq2 = A["P"][:, 6 * 512:8 * 512].bitcast(F32)
        A["rq_ring"] = Ring([(rsr[:, 0, :], [b_rs[0]]), (rq2, [A["b_P"][6], A["b_P"][7]])])
        S.dma("sp", A["cos"], d_cosA[:, :], writes=[A["b_tab"]])
        S.dma("sp", A["sin"], d_sinA[:, :], writes=[A["b_tab"]])

    def attn_cache():
        ckb = A["P"][:, 4 * 512:6 * 512].rearrange("p (c n) -> p c n", c=4)
        bP = A["b_P"]
        S.dma("pool", ckb, d_ck.rearrange("(c p) n -> p c n", p=128), writes=[bP[4], bP[5]], chan=A["b_ckb"])
        S.dma("pool", A["V"][:, 0:4, :], d_cv.rearrange("(c p) n -> p c n", p=128), writes=[A["b_V"][0]], chan=A["b_cvb"])
        for kvh in range(2):
            bk = bank_ring.next()
            pb = ps[:, bk, :].bitcast(BF16)
            for c in range(4):
                tr(pb[:, c * 128:(c + 1) * 128], ckb[:, c, kvh * 128:(kvh + 1) * 128], identb[:], [bP[4], bP[5], b_id], [b_ps[bk]], inc=(c == 3))
            evac(A["kT"][:, kvh, 0:512], pb[:, 0:512], [b_ps[bk]], [A["b_kT"][kvh][0]])

    def qk_items(blk, heads, is_k):
        items = []
        gcol = 1 if is_k else 0
        for (oc, hd) in heads:
            for t in range(3):
                st = {}
                sl = tsl(t)

                def s0(oc=oc, t=t, st=st, sl=sl):
                    ensure(blk)
                    sv = v8(wsl[:, blk.slot, :])
                    bk = A["main_ring"].next()
                    st["bk"] = bk
                    for kc in range(8):
                        mm(ps[:, bk, :], sv[:, kc, oc * 128:(oc + 1) * 128], hT[:, kc, sl], kc == 0, kc == 7,
                           [blk.buf, b_hT[kc][t]], [b_ps[bk]], inc=(kc == 7))
                    u = sq_ring.next()
                    st["sq"] = u
                    act(sqr[:, u, :], ps[:, bk, :], AF.Square, [b_ps[bk]], [b_sq[u]])

                def s1(st=st):
                    u = st["sq"]
                    b2 = A["aux_ring"].next()
                    mm(ps[:, b2, :], onesb[:], sqr[:, u, :], True, True, [b_sq[u], b_id], [b_ps[b2]])
                    rq, rqb = A["rq_ring"].next()
                    st["rq"] = (rq, rqb)
                    act(rq, ps[:, b2, :], AF.Ln, [b_ps[b2], b_id], rqb, bias=epsb[:, 0:1], scale=1.0 / 128)
                    act(rq, rq, AF.Exp, rqb, rqb, scale=-0.5)

                def s2(hd=hd, t=t, st=st, sl=sl):
                    bk = st["bk"]
                    rq, rqb = st["rq"]
                    if t < 2:
                        qi = A["qn_ring"].next()
                        st["qn"] = qi
                        stt(A["qn"][:, qi, :], ps[:, bk, :], qknt[:, gcol:gcol + 1], rq, ALU.mult, ALU.mult,
                            [b_ps[bk], b_const] + rqb, [A["b_qn"][qi]])
                    elif not is_k:
                        stt(A["qT"][:, hd, sl], ps[:, bk, :], qknt[:, gcol:gcol + 1], rq, ALU.mult, ALU.mult,
                            [b_ps[bk], b_const] + rqb, [A["b_qT"][hd][2]])
                    else:
                        bP = A["b_P"]
                        kn32 = A["P"][:, hd * 1024:(hd + 1) * 1024].bitcast(F32)
                        ub = [bP[2 * hd], bP[2 * hd + 1]]
                        stt(kn32, ps[:, bk, :], qknt[:, 1:2], rq, ALU.mult, ALU.mult, [b_ps[bk], b_const] + rqb, ub)
                        cp("act", A["kT"][:, hd, 1536:2048], kn32, ub, [A["b_kT"][hd][3]])
                        b3 = A["aux_ring"].next()
                        for j in range(4):
                            tr(ps[:, b3, j * 128:(j + 1) * 128], kn32[:, j * 128:(j + 1) * 128], identf[:], ub + [b_id], [b_ps[b3]], inc=(j == 3))
                        nkst = A["scr"][:, 0:1024].rearrange("p (j c) -> p j c", j=4)
                        cp("dve", nkst[:, :, hd * 128:(hd + 1) * 128], ps[:, b3, :].rearrange("p (j c) -> p j c", j=4), [b_ps[b3]],
                           [A["b_scr"][0], A["b_scr"][1]])
                        if hd == 1:
                            S.dma("sp", o_nk.rearrange("(j p) c -> p j c", p=128), nkst, reads=[A["b_scr"][0], A["b_scr"][1]], chan=A["b_scr"][0])

                def s3(t=t, st=st, sl=sl):
                    if t >= 2:
                        return
                    qi = st["qn"]
                    b3 = A["aux_ring"].next()
                    st["b3"] = b3
                    mm(ps[:, b3, :], PTb[:], A["qn"][:, qi, :], True, True, [A["b_qn"][qi], b_PT], [b_ps[b3]])
                    t1 = A["t1_ring"].next()
                    st["t1"] = t1
                    tt(tmr[:, t1, :], A["qn"][:, qi, :], A["cos"][:, sl], ALU.mult, [A["b_qn"][qi], A["b_tab"]], [b_tm[t1]])

                def s4(hd=hd, t=t, st=st, sl=sl):
                    if t >= 2:
                        return
                    b3 = st["b3"]; t1 = st["t1"]
                    t2 = 2
                    tt(tmr[:, t2, :], ps[:, b3, :], A["sin"][:, sl], ALU.mult, [b_ps[b3], A["b_tab"]], [b_tm[t2]])
                    if is_k:
                        dst = A["kT"][:, hd, 512 + t * 512:512 + (t + 1) * 512]; db = A["b_kT"][hd][1 + t]
                    else:
                        dst = A["qT"][:, hd, sl]; db = A["b_qT"][hd][t]
                    tt(dst, tmr[:, t1, :], tmr[:, t2, :], ALU.add, [b_tm[t1], b_tm[t2]], [db])
                items.append([s0, s1, s2, s3, s4, t])
        return items

    def v_proj(blk):
        ensure(blk)
        sv = v8(wsl[:, blk.slot, :])
        nvst = A["scr"][:, 1024:2048].rearrange("p (j c) -> p j c", j=4)
        for tc in range(12):
            t = tc // 4
            bk = bank_ring.next()
            for kc in range(8):
                mm(ps[:, bk, 0:256], hT[:, kc, tc * 128:(tc + 1) * 128], sv[:, kc, 256:512], kc == 0, kc == 7,
                   [blk.buf, b_hT[kc][t]], [b_ps[bk]], inc=(kc == 7))
            if tc < 8:
                evac(A["V"][:, 4 + tc, :], ps[:, bk, 0:256], [b_ps[bk]], [A["b_V"][1 + t]])
            else:
                cp("dve", nvst[:, tc - 8, :], ps[:, bk, 0:256], [b_ps[bk]], [A["b_scr"][2], A["b_scr"][3]])
                cp("act", A["V"][:, 4 + tc, :], nvst[:, tc - 8, :], [A["b_scr"][2], A["b_scr"][3]], [A["b_V"][3]])
        if True:
            S.dma("sp", o_nv.rearrange("(j p) c -> p j c", p=128), nvst, reads=[A["b_scr"][2], A["b_scr"][3]], chan=A["b_scr"][2])

    def attention_steps():
        items = []
        for t in range(2):
            for h in range(8):
                items.append(dict(h=h, t=t, q0=t * 512, qn=512, kcs=list(range(12))))
        for pbi in range(2):
            for h in range(0, 8, 2):
                items.append(dict(h=h, t=2, q0=1024 + pbi * 256, qn=512, kcs=[12 + 2 * pbi, 13 + 2 * pbi], pair=True))
        s_ring = Ring([0, 1, 2])
        acc_ring = Ring([(3, 5), (4, 6)])
        micro = []
        for it in items:
            it["st"] = {}
            for j, kc in enumerate(it["kcs"]):
                micro.append((it, j, kc))

        def kbuf(kvh, kc):
            if kc < 4:
                return A["b_kT"][kvh][0]
            if kc < 12:
                return A["b_kT"][kvh][1 + (kc - 4) // 4]
            return A["b_kT"][kvh][3]

        def vbuf(kc):
            if kc < 4:
                return A["b_V"][0]
            if kc < 12:
                return A["b_V"][1 + (kc - 4) // 4]
            return A["b_V"][3]

        def qview(it):
            h = it["h"]; q0 = it["q0"]
            if it.get("pair"):
                return A["qT"][:, h:h + 2, q0:q0 + 256], [A["b_qT"][h][2], A["b_qT"][h + 1][2]]
            return A["qT"][:, h, q0:q0 + it["qn"]], [A["b_qT"][h][it["t"]]]

        def pv2(ap, it):
            return ap.rearrange("p (a q) -> p a q", a=2) if it.get("pair") else ap

        def S_stage(it, j, kc):
            def f():
                h = it["h"]; kvh = h // 4; qn = it["qn"]
                bk = s_ring.next()
                qv, qb = qview(it)
                mm(pv2(ps[:, bk, 0:qn], it), A["kT"][:, kvh, kc * 128:(kc + 1) * 128], qv, True, True,
                   [kbuf(kvh, kc)] + qb, [b_ps[bk]])
                u = A["P_ring"].next()
                it["st"][j] = u
                act(A["P"][:, u * 512:u * 512 + qn], ps[:, bk, 0:qn], AF.Exp, [b_ps[bk]], [A["b_P"][u]], scale=128.0 ** -0.5)
            return f

        def PV_stage(it, j, kc):
            def f():
                h = it["h"]; kvh = h // 4; qn = it["qn"]; q0 = it["q0"]
                nk_ = len(it["kcs"])
                if j == 0:
                    it["acc"] = acc_ring.next()
                ob, dbk = it["acc"]
                u = it["st"][j]
                P = A["P"][:, u * 512:u * 512 + qn]
                mm(ps[:, ob, 0:qn], A["V"][:, kc, kvh * 128:(kvh + 1) * 128], P, j == 0, j == nk_ - 1, [vbuf(kc), A["b_P"][u]], [b_ps[ob]],
                   inc=(j == nk_ - 1))
                mm(ps[:, dbk, 0:qn], onesb[:], P, j == 0, j == nk_ - 1, [A["b_P"][u], b_id], [b_ps[dbk]], inc=True)
                if j == nk_ - 1:
                    ru = rs_ring.next()
                    if it["t"] == 2:
                        act(rsr[:, ru, 0:qn], ps[:, dbk, 0:qn], AF.Ln, [b_ps[dbk]], [b_rs[ru]])
                        act(rsr[:, ru, 0:qn], rsr[:, ru, 0:qn], AF.Exp, [b_rs[ru]], [b_rs[ru]], scale=-1.0)
                    else:
                        S.op("dve", lambda: nc.vector.reciprocal(rsr[:, ru, 0:qn], ps[:, dbk, 0:qn]), [b_ps[dbk]], [b_rs[ru]])
                    if it.get("pair"):
                        dst = hT[:, h:h + 2, q0:q0 + 256]; db = [b_hT[h][2], b_hT[h + 1][2]]
                    else:
                        dst = hT[:, h, q0:q0 + qn]; db = [b_hT[h][it["t"]]]
                    tt(dst, pv2(ps[:, ob, 0:qn], it), pv2(rsr[:, ru, 0:qn], it), ALU.mult, [b_ps[ob], b_rs[ru]], db)
            return f

        LA = 2
        out = []
        n = len(micro)
        for i in range(n + LA):
            if i < n:
                out.append(S_stage(*micro[i]))
            if i >= LA:
                out.append(PV_stage(*micro[i - LA]))
        return out

    def resid_evac(l, wg, oc_base):
        def f(oc, t, bk):
            n = 1 if t < 2 else 0
            o = oc_base + oc
            stt(xT[:, o, tsl(t)], ps[:, bk, :], modT[:, l, wg * 8 + o, n:n + 1], xT[:, o, tsl(t)], ALU.mult, ALU.add,
                [b_ps[bk], b_mod[l][wg], b_xT[o][t]], [b_xT[o][t]])
        return f

    M = {}

    def mlp_alloc():
        AR.reset()
        bank_ring.items = list(range(7))
        M["h1"] = AR.bf(8 * 4 * 512).rearrange("p (u k n) -> p u k n", u=8, k=4)
        M["rt"] = AR.f32(4 * 512).rearrange("p (r n) -> p r n", r=4)
        M["b_h1"] = [Buf(f"h1_{u}") for u in range(8)]
        M["b_rt"] = [Buf(f"rt{u}") for u in range(4)]
        M["h1_ring"] = Ring(range(8)); M["rt_ring"] = Ring(range(4))

    def mlp_steps(l, extras, last_hook=None, pre_extra=None, last_order=(0, 1, 2)):
        steps = []
        units = {}

        def w1_step(g):
            blk = wblk_cols(d_w1[l], g * 512)

            def f():
                us = [M["h1_ring"].next() for _ in range(3)]
                units[g] = us

                def ev(oc, t, bk):
                    r = M["rt_ring"].next()
                    act(M["rt"][:, r, :], ps[:, bk, :], AF.Relu, [b_ps[bk]], [M["b_rt"][r]])
                    tt(M["h1"][:, us[t], oc, :], M["rt"][:, r, :], M["rt"][:, r, :], ALU.mult, [M["b_rt"][r]], [M["b_h1"][us[t]]])
                proj_ws(blk, 4, 8, v8, lambda kc, t: hT[:, kc, tsl(t)], lambda kc, t: [b_hT[kc][t]], ev,
                        after_tile=((lambda t: None) if g == 0 else None))
            return blk, f

        def w2_step(g):
            blk = mkblk([(lambda f: f.rearrange("p (k n) -> p k n", k=4), d_w2[l][g * 512:(g + 1) * 512, :].rearrange("(k p) n -> p k n", p=128))])

            def f():
                us = units[g]
                proj_ws(blk, 8, 4, lambda fl: fl.rearrange("p (k n) -> p k n", k=4), lambda kc, t: M["h1"][:, us[t], kc, :],
                        lambda kc, t: [M["b_h1"][us[t]]], resid_evac(l, 5, 0), after_tile=(last_hook if g == 7 else None),
                        tiles=(last_order if g == 7 else (0, 1, 2)))
            return blk, f
        ex = list(extras)

        def pop_extra():
            if ex:
                grp_ = ex.pop(0)
                for e_ in (grp_ if isinstance(grp_, list) else [grp_]):
                    if callable(e_):
                        steps.append(e_)
                    else:
                        steps.append(mod_blk(e_[0], e_[1], spread=True))
        steps.append(w1_step(0)[1]); pop_extra(); pop_extra()
        steps.append(w1_step(1)[1])
        steps.append(drain)
        for pe_ in (pre_extra or []):
            steps.append(pe_)
        for g in range(8):
            if g == 7:
                steps.append(drain)
            steps.append(w2_step(g)[1])
            if g + 2 < 8:
                steps.append(w1_step(g + 2)[1])
            if g >= 1:
                pop_extra()
        while ex:
            pop_extra()
        steps.append(drain)
        return steps

    def wblk_placeholder():
        return None

    R = {}

    def ret_alloc():
        AR.reset()
        R["qfb"] = AR.bf(2 * 2 * 1024).rearrange("p (v k n) -> p v k n", v=2, k=2)
        R["kT"] = AR.bf(2 * 1024).rearrange("p (k n) -> p k n", k=2)
        R["v"] = AR.bf(8 * 256).rearrange("p (c n) -> p c n", c=8)
        R["kfb"] = AR.bf(8 * 2 * 256).rearrange("p (c v n) -> p c v n", c=8, v=2)
        R["sgT"] = AR.bf(2 * 1024).rearrange("p (k n) -> p k n", k=2)
        R["gT"] = AR.bf(2 * NT).rearrange("p (k n) -> p k n", k=2)
        R["Sst"] = AR.bf(8 * 512).rearrange("p (c k n) -> p c k n", c=8, k=2)
        R["Sfr"] = AR.bf(3 * 512).rearrange("p (r k n) -> p r k n", r=3, k=2)
        R["Sm"] = AR.f32(2 * 512).rearrange("p (d k n) -> p d k n", d=2, k=2)
        R["vT"] = AR.bf(2 * 512).rearrange("p (r n) -> p r n", r=2)
        R["Sm2"] = AR.f32(512).rearrange("p (k n) -> p k n", k=2)
        R["qraw"] = AR.bf(3 * 512).rearrange("p (r n) -> p r n", r=3)
        R["rt"] = AR.f32(4 * 512).rearrange("p (r n) -> p r n", r=4)
        R["sc"] = AR.bf(2 * 128).rearrange("p (r n) -> p r n", r=2)
        R["on"] = AR.bf(2 * 256).rearrange("p (r n) -> p r n", r=2)
        nb = lambda nm, n: [Buf(f"{nm}{i}") for i in range(n)]
        R["b_qfb"] = [nb("qfb0_", 2), nb("qfb1_", 2)]
        R["b_kT"] = [nb("rkT0_", 2), nb("rkT1_", 2)]
        R["b_v"] = nb("rv", 8); R["b_kfb"] = nb("kfb", 8)
        R["b_sgT"] = [nb("sgT0_", 2), nb("sgT1_", 2)]
        R["b_gT"] = [[Buf(f"gT{k}_{t}") for t in range(3)] for k in range(2)]
        R["b_Sst"] = nb("Sst", 8); R["b_Sfr"] = nb("Sfr", 3)
        R["b_Sm"] = [S.dma_buf("Sm0"), S.dma_buf("Sm1")]
        R["b_vT"] = nb("vT", 2); R["b_Sm2"] = S.dma_buf("Sm2"); R["b_qraw"] = nb("qraw", 3); R["b_rt"] = nb("rt", 4)
        R["b_sc"] = nb("sc", 2); R["b_on"] = nb("on", 2)
        R["Sfr_ring"] = Ring(range(3)); R["vT_ring"] = Ring(range(2)); R["qraw_ring"] = Ring(range(3)); R["rt_ring"] = Ring(range(4))
        R["sc_ring"] = Ring(range(2)); R["on_ring"] = Ring(range(2)); R["o_ring"] = Ring([6, 7])
        bank_ring.items = list(range(6))

    def rope_tabs(dkc, t):
        if dkc == 0:
            c = rtab[:, 8 * t:8 * t + 8].unsqueeze(2).broadcast_to([128, 8, 64])
            s_ = rtab[:, 16 + 8 * t:16 + 8 * t + 8].unsqueeze(2).broadcast_to([128, 8, 64])
        else:
            c = rtab[:, 32:96].unsqueeze(1).broadcast_to([128, 8, 64])
            s_ = rtab[:, 96:160].unsqueeze(1).broadcast_to([128, 8, 64])
        return c, s_

    def ret_head_steps(h, wo_hook=None):
        steps = []
        blkA = mkblk([(lambda f: v8(f)[:, :, 0:256], d_wr[:, h * 256:(h + 1) * 256].rearrange("(k p) n -> p k n", p=128)),
                      (lambda f: v8(f)[:, :, 256:512], d_wr[:, 1024 + h * 256:1024 + (h + 1) * 256].rearrange("(k p) n -> p k n", p=128))])
        blkB = mkblk([(lambda f: v8(f)[:, :, 0:256], d_wr[:, 2048 + h * 256:2048 + (h + 1) * 256].rearrange("(k p) n -> p k n", p=128)),
                      (lambda f: v8(f)[:, :, 256:512], d_wr[:, 3072 + h * 256:3072 + (h + 1) * 256].rearrange("(k p) n -> p k n", p=128))])
        blkO = mkblk([(lambda f: f[:, 0:2048].rearrange("p (k n) -> p k n", k=2), d_wro[h * 256:(h + 1) * 256, :].rearrange("(k p) n -> p k n", p=128))])

        def r8(ap):
            return ap.rearrange("p (r c) -> p r c", r=8)

        def r4(ap):
            return ap.rearrange("p (j n) -> p j n", j=4)

        def qk_item(kind, dkc, lt, t, rope):
            oc = dkc if kind == "q" else 2 + dkc
            st = {}
            lsl = slice(lt * 512, (lt + 1) * 512)

            def s0():
                ensure(blkA)
                sv = v8(wsl[:, blkA.slot, :])
                bk = bank_ring.next()
                for kc in range(8):
                    mm(ps[:, bk, :], sv[:, kc, oc * 128:(oc + 1) * 128], hT[:, kc, tsl(t)], kc == 0, kc == 7, [blkA.buf, b_hT[kc][t]], [b_ps[bk]],
                       inc=(kc == 7))
                if rope:
                    r = R["qraw_ring"].next(); st["qraw"] = r
                    cp("act", R["qraw"][:, r, :], ps[:, bk, :], [b_ps[bk]], [R["b_qraw"][r]])
                else:
                    u = R["rt_ring"].next(); st["qr"] = u
                    cp("act", R["rt"][:, u, :], ps[:, bk, :], [b_ps[bk]], [R["b_rt"][u]])

            def s1():
                if rope:
                    r = st["qraw"]
                    b2 = bank_ring.next()
                    mm(ps[:, b2, :], PT2b[:], R["qraw"][:, r, :], True, True, [R["b_qraw"][r], b_PT], [b_ps[b2]])
                    cosv, sinv = rope_tabs(dkc, t)
                    u1 = R["rt_ring"].next(); u2 = R["rt_ring"].next()
                    tt(r8(R["rt"][:, u1, :]), r8(R["qraw"][:, r, :]), cosv, ALU.mult, [R["b_qraw"][r], b_const], [R["b_rt"][u1]])
                    tt(r8(R["rt"][:, u2, :]), r8(ps[:, b2, :]), sinv, ALU.mult, [b_ps[b2], b_const], [R["b_rt"][u2]])
                    tt(R["rt"][:, u1, :], R["rt"][:, u1, :], R["rt"][:, u2, :], ALU.add, [R["b_rt"][u1], R["b_rt"][u2]], [R["b_rt"][u1]])
                    u = u1
                else:
                    u = st["qr"]
                qr = R["rt"][:, u, :]
                if kind == "q":
                    for v_, tab in ((0, DFrow), (1, DBrow)):
                        tt(r4(R["qfb"][:, v_, dkc, lsl]), r4(qr), tab[:, h, :].unsqueeze(1).broadcast_to([128, 4, 128]), ALU.mult,
                           [R["b_rt"][u], b_ret], [R["b_qfb"][dkc][lt]])
                else:
                    cp("act", R["kT"][:, dkc, lsl], qr, [R["b_rt"][u]], [R["b_kT"][dkc][lt]])

            def s2():
                if kind != "k":
                    return
                bk = bank_ring.next()
                pb = ps[:, bk, :].bitcast(BF16)
                for j in range(4):
                    tr(pb[:, j * 128:(j + 1) * 128], R["kT"][:, dkc, lt * 512 + j * 128:lt * 512 + (j + 1) * 128], identb[:],
                       [R["b_kT"][dkc][lt], b_id], [b_ps[bk]], inc=(j == 3))
                c0 = lt * 4
                cb = [R["b_kfb"][c0 + j] for j in range(4)]
                src = pb[:, 0:512].rearrange("p (j n) -> p j n", j=4)
                act(R["kfb"][:, c0:c0 + 4, 0, dkc * 128:(dkc + 1) * 128], src, AF.Identity, [b_ps[bk], b_ret], cb, scale=dk[:, h:h + 1])
                ts(R["kfb"][:, c0:c0 + 4, 1, dkc * 128:(dkc + 1) * 128], src, dk[:, 4 + h:5 + h], None, ALU.mult, None, [b_ps[bk], b_ret], cb)
            return [s0, s1, s2]

        def vg_item(kind, dvc, lt, t):
            oc = dvc if kind == "v" else 2 + dvc
            st = {}
            lsl = slice(lt * 512, (lt + 1) * 512)

            def s0():
                ensure(blkB)
                sv = v8(wsl[:, blkB.slot, :])
                bk = bank_ring.next()
                for kc in range(8):
                    mm(ps[:, bk, :], sv[:, kc, oc * 128:(oc + 1) * 128], hT[:, kc, tsl(t)], kc == 0, kc == 7, [blkB.buf, b_hT[kc][t]], [b_ps[bk]],
                       inc=(kc == 7))
                if kind == "v":
                    r = R["vT_ring"].next(); st["vT"] = r
                    evac(R["vT"][:, r, :], ps[:, bk, :], [b_ps[bk]], [R["b_vT"][r]])
                else:
                    act(R["sgT"][:, dvc, lsl], ps[:, bk, :], AF.Silu, [b_ps[bk]], [R["b_sgT"][dvc][lt]])

            def s1():
                if kind != "v":
                    return
                r = st["vT"]
                bk = bank_ring.next()
                pb = ps[:, bk, :].bitcast(BF16)
                for j in range(4):
                    tr(pb[:, j * 128:(j + 1) * 128], R["vT"][:, r, j * 128:(j + 1) * 128], identb[:], [R["b_vT"][r], b_id], [b_ps[bk]], inc=(j == 3))
                c0 = lt * 4
                evac(R["v"][:, c0:c0 + 4, dvc * 128:(dvc + 1) * 128], pb[:, 0:512].rearrange("p (j n) -> p j n", j=4), [b_ps[bk]],
                     [R["b_v"][c0 + j] for j in range(4)])
            return [s0, s1, None]

        def state_init(d, sq):
            def f():
                if sq["pb"] is None:
                    S.dma("sp", R["Sm"][:, d], d_sr[d, h].rearrange("(k p) n -> p k n", p=128), writes=[R["b_Sm"][d]])
                else:
                    S.op("dve", lambda: nc.vector.memset(R["Sm"][:, d], 0.0), [], [R["b_Sm"][d]])
            return f

        def state_update(d, lc, last, sq, src=None, dst=None):
            if last and sq["pb"] is None:
                return
            if src is None:
                src = dst = (R["Sm"][:, d], R["b_Sm"][d])
            bk = bank_ring.next()
            pv = ps[:, bk, :].rearrange("p (k n) -> p k n", k=2)
            for dkc in range(2):
                mm(pv[:, dkc, :], R["kfb"][:, lc, d, dkc * 128:(dkc + 1) * 128], R["v"][:, lc, :], True, True,
                   [R["b_kfb"][lc], R["b_v"][lc]], [b_ps[bk]], inc=(dkc == 1))
            stt(dst[0], src[0], cdt[:, 4 * d + h:4 * d + h + 1], pv, ALU.mult, ALU.add,
                [src[1], b_ps[bk], b_ret], [dst[1]])
            if last:
                S.dma("sp", o_ns[sq["pb"], d, h].rearrange("(k p) n -> p k n", p=128), dst[0], reads=[dst[1]])

        def bwd_steps(sq):
            out = [state_init(1, sq)]
            nch = sq["nch"]
            for i in range(nch):
                def f(i=i):
                    c = nch - 1 - i
                    lc = sq["lc0"] + c
                    cp("act", R["Sst"][:, lc], R["Sm"][:, 1], [R["b_Sm"][1]], [R["b_Sst"][lc]])
                    state_update(1, lc, i == nch - 1, sq)
                out.append(f)
            return out

        def fwd_items(sq):
            items = []
            nch = sq["nch"]
            for c in range(nch):
                st = {}
                lc = sq["lc0"] + c
                ltok = lc * 128
                lt = lc // 4
                gtok = sq["tok0"] + c * 128
                gt = gtok // 512

                def b0(c=c, st=st, lc=lc):
                    M = [(R["Sm"][:, 0], R["b_Sm"][0]), (R["Sm2"], R["b_Sm2"])]
                    if c == 0:
                        state_init(0, sq)()
                    cur = c % 2
                    r = R["Sfr_ring"].next(); st["sf"] = r
                    cp("act", R["Sfr"][:, r], M[cur][0], [M[cur][1]], [R["b_Sfr"][r]])
                    state_update(0, lc, c == nch - 1, sq, src=M[cur], dst=M[1 - cur])

                def b1(st=st, lc=lc, ltok=ltok, lt=lt):
                    bk = bank_ring.next()
                    for dkc in range(2):
                        mm(ps[:, bk, 0:128], R["kT"][:, dkc, ltok:ltok + 128], R["qfb"][:, 0, dkc, ltok:ltok + 128], dkc == 0, dkc == 1,
                           [R["b_kT"][dkc][lt], R["b_qfb"][dkc][lt]], [b_ps[bk]], inc=(dkc == 1))
                    si = R["sc_ring"].next(); st["si"] = si
                    tt(R["sc"][:, si, :], ps[:, bk, 0:128], maskt[:, h, :], ALU.mult, [b_ps[bk], b_ret], [R["b_sc"][si]])

                def b2(st=st, lc=lc, ltok=ltok, lt=lt):
                    si = st["si"]; sf = st["sf"]
                    bk = R["o_ring"].next()
                    mm(ps[:, bk, 0:256], R["sc"][:, si, :], R["v"][:, lc, :], True, False, [R["b_sc"][si], R["b_v"][lc]], [b_ps[bk]], inc=False)
                    for dkc in range(2):
                        mm(ps[:, bk, 0:256], R["qfb"][:, 0, dkc, ltok:ltok + 128], R["Sfr"][:, sf, dkc, :], False, False,
                           [R["b_qfb"][dkc][lt], R["b_Sfr"][sf]], [b_ps[bk]], inc=False)
                    for dkc in range(2):
                        mm(ps[:, bk, 0:256], R["qfb"][:, 1, dkc, ltok:ltok + 128], R["Sst"][:, lc, dkc, :], False, dkc == 1,
                           [R["b_qfb"][dkc][lt], R["b_Sst"][lc]], [b_ps[bk]], inc=(dkc == 1))
                    su = small_ring.next()
                    st["su"] = su; st["obk"] = bk
                    sm = smallt[:, su, :]
                    S.op("dve", lambda: nc.vector.bn_stats(sm[:, 0:6], ps[:, bk, 0:256]), [b_ps[bk]], [b_small[su]])
                    S.op("dve", lambda: nc.vector.bn_aggr(sm[:, 6:8], sm[:, 0:6]), [b_small[su]], [b_small[su]])
                    ts(sm[:, 10:11], sm[:, 6:7], -1.0, None, ALU.mult, None, [b_small[su]], [b_small[su]])

                def b2b(st=st):
                    su = st["su"]; bk = st["obk"]
                    sm = smallt[:, su, :]
                    act(sm[:, 8:9], sm[:, 7:8], AF.Ln, [b_small[su]], [b_small[su]], bias=epsb[:, 0:1], scale=1.0)
                    act(sm[:, 8:9], sm[:, 8:9], AF.Exp, [b_small[su]], [b_small[su]], scale=-0.5)
                    act(sm[:, 9:10], sm[:, 10:11], AF.Identity, [b_small[su]], [b_small[su]], scale=sm[:, 8:9])
                    oi = R["on_ring"].next(); st["oi"] = oi
                    act(R["on"][:, oi, :], ps[:, bk, 0:256], AF.Identity, [b_ps[bk], b_small[su]], [R["b_on"][oi]], bias=sm[:, 9:10], scale=sm[:, 8:9])

                def b3(st=st, ltok=ltok, lt=lt, gtok=gtok, gt=gt):
                    oi = st["oi"]
                    bk = bank_ring.next()
                    pb = ps[:, bk, :].bitcast(BF16)
                    for dvc in range(2):
                        tr(pb[:, dvc * 128:(dvc + 1) * 128], R["on"][:, oi, dvc * 128:(dvc + 1) * 128], identb[:], [R["b_on"][oi], b_id],
                           [b_ps[bk]], inc=(dvc == 1))
                    tt(R["gT"][:, :, gtok:gtok + 128], pb[:, 0:256].rearrange("p (k n) -> p k n", k=2), R["sgT"][:, :, ltok:ltok + 128], ALU.mult,
                       [b_ps[bk], R["b_sgT"][0][lt], R["b_sgT"][1][lt]], [R["b_gT"][0][gt], R["b_gT"][1][gt]])
                items.append([b0, b1, b2, b2b, b3])
            return items

        partS = dict(tiles=[0, 1], rope=True, seqs=[dict(tok0=0, nch=8, pb=None, lc0=0)])
        partP = dict(tiles=[2], rope=False, seqs=[dict(tok0=1024, nch=2, pb=0, lc0=0), dict(tok0=1280, nch=2, pb=1, lc0=2)])

        def p1_lists(part, lt_order):
            first = []; second = []
            for lt in lt_order:
                t = part["tiles"][lt]
                for dkc in range(2):
                    first.append(qk_item("k", dkc, lt, t, part["rope"]))
                for dvc in range(2):
                    first.append(vg_item("v", dvc, lt, t))
            for lt, t in enumerate(part["tiles"]):
                for dkc in range(2):
                    second.append(qk_item("q", dkc, lt, t, part["rope"]))
                for dvc in range(2):
                    second.append(vg_item("g", dvc, lt, t))
            return first, second

        def flat(ss):
            return [c for st_ in ss for c in st_]

        def zipsteps(a_, b_):
            out = []
            for i in range(max(len(a_), len(b_))):
                if i < len(a_):
                    out.extend(a_[i])
                if i < len(b_):
                    out.extend(b_[i])
            return out

        def p1b_with_bwd(part, second):
            p1b = pipeline_steps(second, [0, 1, 2])
            bw = []
            for sq in part["seqs"]:
                bw.extend([[c] for c in bwd_steps(sq)])
            return zipsteps(p1b, bw)

        firstP, secondP = p1_lists(partP, [0])
        steps.extend(flat(pipeline_steps(firstP, [0, 1, 2])))
        steps.extend(p1b_with_bwd(partP, secondP))
        fitems = []
        for sq in partP["seqs"]:
            fitems.extend(fwd_items(sq))
        fwdP = pipeline_steps(fitems, [0, 0, 1, 2, 3])
        def s_items(lt):
            t = partS["tiles"][lt]
            kv = [qk_item("k", dkc, lt, t, True) for dkc in range(2)] + [vg_item("v", dvc, lt, t) for dvc in range(2)]
            qg = []
            for i in range(2):
                qg.append(qk_item("q", i, lt, t, True)); qg.append(vg_item("g", i, lt, t))
            return kv, qg
        kv1, qg1 = s_items(1)
        kv0, qg0 = s_items(0)
        stepsA = pipeline_steps(kv1 + qg1, [0, 1, 2])
        steps.extend(zipsteps(fwdP, stepsA))
        steps.extend(flat(pipeline_steps(kv0, [0, 1, 2])))
        steps.extend(p1b_with_bwd(partS, qg0))
        steps.append(lambda: (release(blkA), release(blkB)))
        p3S = pipeline_steps(fwd_items(partS["seqs"][0]), [0, 0, 1, 2, 3])

        def wo_sv():
            return wsl[:, blkO.slot, 0:2048].rearrange("p (k n) -> p k n", k=2)

        def wo_prep():
            ensure(blkO)
            sv = wo_sv()
            for kc in range(2):
                ts(sv[:, kc, :], sv[:, kc, :], gnwt[:, 2 * h + kc:2 * h + kc + 1], None, ALU.mult, None, [blkO.buf, b_const], [blkO.buf])

        def grp(oc, t):
            sv = wo_sv()
            bk = bank_ring.next()
            for kc in range(2):
                mm(ps[:, bk, :], sv[:, kc, oc * 128:(oc + 1) * 128], R["gT"][:, kc, tsl(t)], kc == 0, kc == 1,
                   [blkO.buf, R["b_gT"][kc][t]], [b_ps[bk]], inc=(kc == 1))
            resid_evac(1, 2, 0)(oc, t, bk)

        if wo_hook is None:
            parts = [wo_prep]
            gl = [(oc, t) for oc in range(8) for t in range(3)]
            for i in range(0, 24, 2):
                parts.append(lambda i=i: (grp(*gl[i]), grp(*gl[i + 1])))
            parts.append(lambda: release(blkO))
            return steps, p3S, parts

        def wo_step():
            wo_prep()
            pending = None
            for t in range(3):
                for oc in range(8):
                    grp(oc, t)
                    if pending is not None and oc == 1:
                        wo_hook(pending)
                        pending = None
                pending = t
            wo_hook(pending)
            release(blkO)
        return steps, p3S, [wo_step]

    FIN = {}

    def fin_alloc():
        if "yst" in FIN:
            return
        o = 41984
        FIN["yst"] = arena[:, o // 2:o // 2 + 3 * 2048].bitcast(F32).rearrange("p (r n) -> p r n", r=3)
        FIN["b_y"] = [S.dma_buf(f"yst{i}") for i in range(3)]
        FIN["ring"] = Ring(range(3))

    def fin_tile(t, dump=False):
        def f():
            yst = FIN["yst"]; b_y = FIN["b_y"]
            sl = tsl(t)
            bk = bank_ring.next()
            for kc in range(8):
                u = sq_ring.next()
                if kc % 2 == 0:
                    act(sqr[:, u, :], xT[:, kc, sl], AF.Square, [b_xT[kc][t]], [b_sq[u]])
                else:
                    tt(sqr[:, u, :], xT[:, kc, sl], xT[:, kc, sl], ALU.mult, [b_xT[kc][t]], [b_sq[u]])
                mm(ps[:, bk, :], onesb[:], sqr[:, u, :], kc == 0, kc == 7, [b_sq[u], b_id], [b_ps[bk]], inc=True)
            ru = nrs_ring.next()
            act(rsr[:, ru, :], ps[:, bk, :], AF.Ln, [b_ps[bk], b_id], [b_rs[ru]], bias=epsb[:, 0:1], scale=1.0 / D)
            act(rsr[:, ru, :], rsr[:, ru, :], AF.Exp, [b_rs[ru]], [b_rs[ru]], scale=-0.5)
            for kc in range(8):
                if dump:
                    break
                stt(xT[:, kc, sl], xT[:, kc, sl], fngt[:, kc:kc + 1], rsr[:, ru, :], ALU.mult, ALU.mult, [b_xT[kc][t], b_rs[ru], b_const],
                    [b_xT[kc][t]])
            for j in range(4):
                tok = t * 512 + j * 128
                yi = FIN["ring"].next()
                for half in range(2):
                    b2 = bank_ring.next()
                    for q in range(4):
                        kc = half * 4 + q
                        tr(ps[:, b2, q * 128:(q + 1) * 128], xT[:, kc, tok:tok + 128], identf[:], [b_xT[kc][t], b_id], [b_ps[b2]], inc=(q == 3))
                    evac(yst[:, yi, half * 512:(half + 1) * 512], ps[:, b2, :], [b_ps[b2]], [b_y[yi]])
                dst = o_ys[tok:tok + 128, :] if t < 2 else o_yp[tok - 1024:tok - 1024 + 128, :]
                S.dma("sp", dst, yst[:, yi, :], reads=[b_y[yi]])
        return f

    def final_phase(dump=False):
        fin_alloc()
        for t in range(3):
            fin_tile(t, dump)()

    epsb = nc.alloc_sbuf_tensor("epsb", [128, 1], F32)
    setup()
    steps = []
    marks = {}
    snaps = {}

    def snap(name):
        return lambda: snaps.__setitem__(name, S.snapshot())

    def wsnap(name):
        return lambda: S.wait_snapshot(snaps[name])
    n00 = norm_items(0, 0)
    steps.append(li_alloc)
    steps.append(setup_consts)
    steps.append(li_dma(0))
    steps.append(lambda: S.wait_tokens("pool", [LI["b"][0].w]))
    steps.append(prefetch)
    steps.append(li_dma(1))
    steps.append(setup_pt)
    steps.append(li_tr(0)); steps.append(li_tr(1)); steps.append(li_dma(2))
    steps.append(n00[0][0])
    steps.append(setup_silu)
    steps.append(mod_blk(0, 0))
    steps.append(li_tr(2))
    steps.append(snap("s0"))
    steps.append(n00[1][0])
    for hb in range(1, 4):
        steps.append(mod_blk(0, hb))
    steps.append(n00[2][0])
    steps.append(n00[0][1]); steps.append(n00[1][1]); steps.append(n00[2][1])
    steps.append(wsnap("s0"))
    steps.append(attn_alloc)
    marks['attn_alloc'] = len(steps)
    bq0 = wblk_cols(d_wqkv, 0); bq1 = wblk_cols(d_wqkv, 512); bkv = wblk_cols(d_wqkv, 1024)
    qitems = qk_items(bq0, [(i, i) for i in range(4)], False) + qk_items(bq1, [(i, 4 + i) for i in range(4)], False) \
        + qk_items(bkv, [(0, 0), (1, 1)], True)
    qitems = sorted(qitems, key=lambda it: it[5])
    steps.extend(run_pipeline([it[:5] for it in qitems], [0, 1, 2, 3, 4]))
    steps.append(attn_cache)
    steps.append(lambda: v_proj(bkv))
    steps.append(lambda: (release(bq0), release(bq1), release(bkv)))
    marks['qkv_done'] = len(steps)
    att = attention_steps()
    ex0 = [mod_blk(0, hb, bank=7) for hb in range(4, 10)]
    stride = max(1, len(att) // 8)
    for i, f in enumerate(att):
        steps.append(f)
        if ex0 and i % stride == stride - 1:
            steps.append(ex0.pop(0))
    steps.extend(ex0)
    marks['attn_done'] = len(steps)
    steps.append(snap("s1"))
    hk01, tail01 = norm_hooks(0, 1)
    for half in range(2):
        blk = wblk_cols(d_wo, half * 512)
        steps.append(lambda blk=blk, half=half: proj_ws(blk, 4, 8, v8, lambda kc, t: hT[:, kc, tsl(t)], lambda kc, t: [b_hT[kc][t]],
                                                        resid_evac(0, 2, half * 4), after_tile=(hk01 if half == 1 else None)))
    marks['l0_mixer_done'] = len(steps)
    steps.append(tail01)
    steps.append(wsnap("s1"))
    steps.append(mlp_alloc)
    rp = setup_ret_pieces()
    extras0 = [(0, 10), (0, 11)] + [[(1, i), rp[i]] for i in range(6)]
    hk10, tail10 = norm_hooks(1, 0, order=(2, 0, 1))
    steps.extend(mlp_steps(0, extras0, last_hook=hk10, last_order=(2, 0, 1)))
    marks['l0_done'] = len(steps)
    steps.append(snap("s2"))
    steps.append(tail10)
    steps.append(wsnap("s2"))
    steps.append(ret_alloc)
    hk11, tail11 = norm_hooks(1, 1)
    pend = []
    carry = []
    for h in range(4):
        mb = mod_blk(1, 6 + h)
        body, p3S, wparts = ret_head_steps(h, wo_hook=(hk11 if h == 3 else None))
        hs = [mb] + body
        i = 0
        for st_ in carry:
            steps.extend(st_)
            if i < len(hs):
                steps.append(hs[i]); i += 1
        for st_ in hs[i:]:
            steps.append(st_)
            if pend:
                steps.append(pend.pop(0))
        steps.extend(pend)
        steps.extend([c for st_ in p3S[:7] for c in st_])
        carry = p3S[7:]
        pend = wparts
        marks[f'ret_h{h}'] = len(steps)
    steps.extend([c for st_ in carry for c in st_])
    steps.extend(pend)
    marks['l1_mixer_done'] = len(steps)
    steps.append(snap("s3"))
    steps.append(tail11)
    steps.append(wsnap("s3"))
    steps.append(mlp_alloc)
    def fin_hook(t):
        if t == 1:
            fin_tile(0)()
        elif t == 2:
            fin_tile(1)()
    steps.append(fin_alloc)
    steps.extend(mlp_steps(1, [(1, 10), (1, 11)], last_hook=fin_hook))
    marks['l1_done'] = len(steps)
    steps.append(fin_tile(2))
    if dbg:
        print('marks', marks)
    if nsteps is not None:
        if isinstance(nsteps, str):
            nsteps = marks[nsteps]
        steps = steps[:nsteps] + [lambda: S.full_barrier(), lambda: final_phase(dump=True)]
    print('nsteps', len(steps)) if dbg else None
    for f in steps:
        f()
    S.full_barrier()
    return nc


_CACHE = {}


def _rope_tables():
    n = np.arange(1024)
    rows = (n // 64).astype(np.float64)
    cols = (n % 64).astype(np.float64)
    f32_ = 10000.0 ** (-np.arange(32, dtype=np.float32) / 32).astype(np.float32)
    ang = np.zeros((128, 1024), np.float32)
    for p in range(128):
        fr = f32_[p % 32]
        ang[p] = (rows if p < 64 else cols).astype(np.float32) * fr
    cosA = np.cos(ang).astype(np.float32)
    sinA = np.sin(ang).astype(np.float32)
    P = np.zeros((128, 128), np.float32)
    for m in range(128):
        if m % 64 < 32:
            P[m, m + 32] = -1.0
        else:
            P[m, m - 32] = 1.0
    PT = np.ascontiguousarray(P.T)
    f64_ = 10000.0 ** (-np.arange(64, dtype=np.float32) / 64).astype(np.float32)
    rc = np.zeros((128, 8, 2, 64), np.float32)
    rs = np.zeros((128, 8, 2, 64), np.float32)
    for tc in range(8):
        for p in range(128):
            tok = tc * 128 + p
            a0 = np.float32(tok // 64) * f64_
            a1 = np.float32(tok % 64) * f64_
            rc[p, tc, 0] = np.cos(a0); rc[p, tc, 1] = np.cos(a1)
            rs[p, tc, 0] = np.sin(a0); rs[p, tc, 1] = np.sin(a1)
    rtab = np.zeros((128, 160), np.float32)
    P2 = np.zeros((128, 128), np.float32)
    for p in range(128):
        fr = f64_[p % 64]
        rtab[p, 0:16] = np.cos(np.arange(16, dtype=np.float32) * fr); rtab[p, 16:32] = np.sin(np.arange(16, dtype=np.float32) * fr)
        rtab[p, 32:96] = np.cos(np.arange(64, dtype=np.float32) * fr); rtab[p, 96:160] = np.sin(np.arange(64, dtype=np.float32) * fr)
        if p < 64:
            P2[p, p + 64] = -1.0
        else:
            P2[p, p - 64] = 1.0
    rconst = np.zeros((128, 641), np.float32)
    jj = np.arange(128, dtype=np.float32)[:, None]; ii = np.arange(128, dtype=np.float32)[None, :]
    rconst[:, 0:128] = ii - jj
    rconst[:, 128:256] = (ii >= jj).astype(np.float32)
    rconst[:, 256:384] = (jj > ii).astype(np.float32)
    rconst[:, 384:512] = ii + 1.0
    rconst[:, 512:640] = 128.0 - ii
    rconst[:, 640] = np.arange(128, dtype=np.float32)
    return cosA, sinA, PT, rtab, np.ascontiguousarray(P2.T), rconst


def _pl(v):
    v = np.asarray(v, np.float32)
    lead = v.shape[:-1]
    r = v.reshape(lead + (8, 128))
    r = np.moveaxis(r, -1, 0)
    return np.ascontiguousarray(r)


def kernel(x_prompt, x_sample, cache_k, cache_v, state_ret, c, c_ctx, w_mod, b_mod, norm_g, attn_w_qkv, attn_q_norm,
           attn_k_norm, attn_w_o, ret_w_qkvg, ret_decay_logit, ret_gn_w, ret_w_o, mlp_w1, mlp_w2, final_norm_g):
    f = lambda a: np.ascontiguousarray(np.asarray(a, dtype=np.float32))
    if "nc" not in _CACHE:
        _CACHE["nc"] = build()
        _CACHE["tabs"] = _rope_tables()
    nc = _CACHE["nc"]
    cosA, sinA, PT, rtab_h, PT2, rconst = _CACHE["tabs"]
    x_prompt = f(x_prompt); x_sample = f(x_sample); cache_k = f(cache_k); cache_v = f(cache_v); state_ret = f(state_ret)
    c = f(c); c_ctx = f(c_ctx)
    shared = {
        "w_mod": f(w_mod),
        "bmod": np.ascontiguousarray(f(b_mod).reshape(2, 48, 128).transpose(2, 0, 1).reshape(128, 96)),
        "ng": np.ascontiguousarray(_pl(norm_g).reshape(128, 32)),
        "fng": np.ascontiguousarray(_pl(final_norm_g).reshape(128, 8)),
        "wqkv": f(attn_w_qkv)[0],
        "qkn": np.ascontiguousarray(np.stack([f(attn_q_norm)[0], f(attn_k_norm)[0]], axis=1)),
        "wo": f(attn_w_o)[0],
        "wr": f(ret_w_qkvg)[0],
        "dlog": np.ascontiguousarray(np.broadcast_to(f(ret_decay_logit)[0].reshape(1, 8), (128, 8))),
        "gnw": np.ascontiguousarray(_pl(f(ret_gn_w)[0]).reshape(128, 8)),
        "wro": f(ret_w_o)[0],
        "w1": f(mlp_w1), "w2": f(mlp_w2),
        "cosA": cosA, "sinA": sinA, "PTm": PT, "rtab": rtab_h, "PT2m": PT2, "rconst": rconst,
    }
    in_maps = []
    for i in range(NCORES):
        cc = np.stack([c_ctx, c[i]], axis=0)
        cT = np.ascontiguousarray(cc.reshape(2, 8, 128).transpose(2, 1, 0).reshape(128, 16))
        m = dict(shared)
        m.update({
            "xs": x_sample[i], "xp": np.ascontiguousarray(x_prompt[2 * i:2 * i + 2].reshape(512, D)),
            "ck": np.ascontiguousarray(cache_k[i, 0].reshape(512, 256)), "cv": np.ascontiguousarray(cache_v[i, 0].reshape(512, 256)),
            "sr": np.ascontiguousarray(state_ret[i, 0]), "cT": cT,
        })
        in_maps.append(m)
    res = run_bass_kernel_spmd(nc, in_maps, core_ids=list(range(NCORES)))
    rr = res.results
    y_prompt = np.concatenate([r["yp"].reshape(2, 256, D) for r in rr], axis=0)
    y_sample = np.stack([r["ys"] for r in rr], axis=0)
    nk = np.concatenate([r["nk"].reshape(2, 1, 256, 2, 128) for r in rr], axis=0)
    nv = np.concatenate([r["nv"].reshape(2, 1, 256, 2, 128) for r in rr], axis=0)
    ns = np.concatenate([r["ns"].reshape(2, 1, 2, 4, 256, 256) for r in rr], axis=0)
    return (y_prompt.astype(np.float32), y_sample.astype(np.float32), nk.astype(np.float32), nv.astype(np.float32), ns.astype(np.float32))
```

```python
import numpy as np
import concourse.bass as bass
import concourse.mybir as mybir
from concourse.bass_utils import run_bass_kernel_spmd

F32 = mybir.dt.float32
BF16 = mybir.dt.bfloat16
I32 = mybir.dt.int32
AF = mybir.ActivationFunctionType
ALU = mybir.AluOpType

NCORES = 8
D = 1024
NT = 1536
EPS = 1e-6
NSLOT = 5
SAME_ENGINE_SYNC = True


class Buf:
    __slots__ = ("name", "w", "r", "sem", "nd", "excl")

    def __init__(self, name, sem=None):
        self.name = name
        self.excl = False
        self.w = None
        self.r = []
        self.sem = sem
        self.nd = 0


class Eng:
    def __init__(self, name, eng, sem, is_pe=False):
        self.name = name
        self.eng = eng
        self.sem = sem
        self.count = 0
        self.seen = {}
        self.is_pe = is_pe
        self.pending = False


class Sched:
    def __init__(self, nc):
        self.nc = nc
        self.sems = {}
        self.engs = {}
        self.dma_toks = []
        self.other_toks = []

    def add_engine(self, name, eng, is_pe=False):
        sem = self.nc.alloc_semaphore(name=f"sem_{name}")
        e = Eng(name, eng, sem, is_pe)
        self.engs[name] = e
        self.sems[name] = sem
        return e

    def dma_buf(self, name):
        sem = self.nc.alloc_semaphore(name=f"dsem_{name}")
        self.sems["d:" + name] = sem
        return Buf(name, sem="d:" + name)

    def _wait(self, e, deps):
        best = {}
        for (k, v) in deps:
            if k == e.name and (e.is_pe or not SAME_ENGINE_SYNC):
                continue
            if best.get(k, 0) < v:
                best[k] = v
        for k, v in best.items():
            if e.seen.get(k, 0) < v:
                e.eng.wait_ge(self.sems[k], v)
                e.seen[k] = v

    @staticmethod
    def _deps(reads, writes):
        deps = []
        for b in reads:
            if b.w is not None:
                deps.append(b.w)
            if b.excl:
                deps.extend(b.r)
        for b in writes:
            if b.w is not None:
                deps.append(b.w)
            deps.extend(b.r)
        return deps

    def op(self, ename, fn, reads=(), writes=(), inc=True):
        e = self.engs[ename]
        self._wait(e, self._deps(reads, writes))
        ins = fn()
        tok = (e.name, e.count + 1)
        if inc:
            ins.then_inc(e.sem, 1)
            e.count += 1
            e.pending = False
        else:
            e.pending = True
        for b in writes:
            b.w = tok
            b.r = []
        for b in reads:
            if not b.r or b.r[-1] != tok:
                b.r.append(tok)
        return ins

    def dma(self, qname, out, in_, reads=(), writes=(), chan=None, arena=True):
        e = self.engs[qname]
        self._wait(e, self._deps(reads, writes))
        cb = chan if chan is not None else (list(writes) + list(reads))[0]
        assert cb.sem is not None, cb.name
        ins = e.eng.dma_start(out=out, in_=in_)
        ins.then_inc(self.sems[cb.sem], 16)
        cb.nd += 1
        tok = (cb.sem, 16 * cb.nd)
        for b in writes:
            b.w = tok
            b.r = []
        for b in reads:
            b.r.append(tok)
        (self.dma_toks if arena else self.other_toks).append(tok)
        return ins

    def wait_tokens(self, ename, toks):
        self._wait(self.engs[ename], toks)

    def snapshot(self):
        toks = list(self.dma_toks)
        for e in self.engs.values():
            assert not e.pending, e.name
            if e.count > 0:
                toks.append((e.name, e.count))
        return toks

    def wait_snapshot(self, toks):
        for e in self.engs.values():
            self._wait(e, toks)

    def full_barrier(self):
        toks = list(self.dma_toks) + list(self.other_toks)
        for e in self.engs.values():
            assert not e.pending, e.name
            if e.count > 0:
                toks.append((e.name, e.count))
        for e in self.engs.values():
            self._wait(e, toks)


class Ring:
    def __init__(self, items):
        self.items = list(items)
        self.i = 0

    def next(self):
        x = self.items[self.i % len(self.items)]
        self.i += 1
        return x


class Blk:
    def __init__(self, parts):
        self.parts = parts
        self.slot = None
        self.buf = None
        self.idx = None


def run_pipeline(items, offsets):
    out = []
    n = len(items)
    if n == 0:
        return out
    maxo = max(offsets)
    for s in range(n + maxo):
        for k, off in enumerate(offsets):
            i = s - off
            if 0 <= i < n and items[i][k] is not None:
                out.append(items[i][k])
    return out


def pipeline_steps(items, offsets):
    out = []
    n = len(items)
    if n == 0:
        return out
    for s_ in range(n + max(offsets)):
        cur = []
        for k, off in enumerate(offsets):
            i = s_ - off
            if 0 <= i < n and items[i][k] is not None:
                cur.append(items[i][k])
        out.append(cur)
    return out


def build(nsteps=None, dbg=False):
    nc = bass.Bass("TRN2", target_bir_lowering=False)
    S = Sched(nc)
    S.add_engine("pe", nc.tensor, True)
    S.add_engine("act", nc.scalar)
    S.add_engine("dve", nc.vector)
    S.add_engine("pool", nc.gpsimd)
    S.add_engine("sp", nc.sync)

    def din(name, shape):
        return nc.dram_tensor(name, shape, F32, kind="ExternalInput").ap()

    def dout(name, shape):
        return nc.dram_tensor(name, shape, F32, kind="ExternalOutput").ap()

    d_xs = din("xs", [1024, D]); d_xp = din("xp", [512, D])
    d_ck = din("ck", [512, 256]); d_cv = din("cv", [512, 256])
    d_sr = din("sr", [2, 4, 256, 256])
    d_cT = din("cT", [128, 16])
    d_wmod = din("w_mod", [2, D, 6 * D]); d_bmod = din("bmod", [128, 96])
    d_ng = din("ng", [128, 32]); d_fng = din("fng", [128, 8])
    d_wqkv = din("wqkv", [D, 1536]); d_qkn = din("qkn", [128, 2]); d_wo = din("wo", [D, D])
    d_wr = din("wr", [D, 4096]); d_dlog = din("dlog", [128, 8]); d_gnw = din("gnw", [128, 8])
    d_wro = din("wro", [D, D])
    d_w1 = din("w1", [2, D, 4096]); d_w2 = din("w2", [2, 4096, D])
    d_cosA = din("cosA", [128, 1024]); d_sinA = din("sinA", [128, 1024]); d_PT = din("PTm", [128, 128])
    d_rtab = din("rtab", [128, 160]); d_PT2 = din("PT2m", [128, 128]); d_rconst = din("rconst", [128, 641])
    o_yp = dout("yp", [512, D]); o_ys = dout("ys", [1024, D])
    o_nk = dout("nk", [512, 256]); o_nv = dout("nv", [512, 256])
    o_ns = dout("ns", [2, 2, 4, 256, 256])

    xT = nc.alloc_sbuf_tensor("xT", [128, 8, NT], F32)
    hT = nc.alloc_sbuf_tensor("hT", [128, 8, NT], BF16)
    wsl = nc.alloc_sbuf_tensor("wsl", [128, NSLOT, 4096], BF16)
    sqr = nc.alloc_sbuf_tensor("sqr", [128, 4, 512], BF16)
    rsr = nc.alloc_sbuf_tensor("rsr", [128, 4, 512], F32)
    tmr = nc.alloc_sbuf_tensor("tmr", [128, 3, 512], F32)
    identf = nc.alloc_sbuf_tensor("identf", [128, 128], F32)
    identb = nc.alloc_sbuf_tensor("identb", [128, 128], BF16)
    onesb = nc.alloc_sbuf_tensor("onesb", [128, 128], BF16)
    PTb = nc.alloc_sbuf_tensor("PTb", [128, 128], BF16)
    PT2b = nc.alloc_sbuf_tensor("PT2b", [128, 128], BF16)
    rtab = nc.alloc_sbuf_tensor("rtab_sb", [128, 160], F32)
    DFrow = nc.alloc_sbuf_tensor("DFrow", [128, 4, 128], F32)
    DBrow = nc.alloc_sbuf_tensor("DBrow", [128, 4, 128], F32)
    modT = nc.alloc_sbuf_tensor("modT", [128, 2, 48, 2], F32)
    bmodt = nc.alloc_sbuf_tensor("bmodt", [128, 2, 48], F32)
    ngt = nc.alloc_sbuf_tensor("ngt", [128, 2, 2, 8], F32)
    fngt = nc.alloc_sbuf_tensor("fngt", [128, 8], F32)
    gsT = nc.alloc_sbuf_tensor("gsT", [128, 2, 2, 8, 2], F32)
    cTt = nc.alloc_sbuf_tensor("cTt", [128, 8, 2], F32)
    scT = nc.alloc_sbuf_tensor("scT", [128, 8, 2], BF16)
    qknt = nc.alloc_sbuf_tensor("qknt", [128, 2], F32)
    dlt = nc.alloc_sbuf_tensor("dlt", [128, 8], F32)
    Lt = nc.alloc_sbuf_tensor("Lt", [128, 8], F32)
    nLt = nc.alloc_sbuf_tensor("nLt", [128, 8], F32)
    gnwt = nc.alloc_sbuf_tensor("gnwt", [128, 8], F32)
    pidx_i = nc.alloc_sbuf_tensor("pidx_i", [128, 4], I32)
    pidx = nc.alloc_sbuf_tensor("pidx", [128, 4], F32)
    dq = nc.alloc_sbuf_tensor("dq", [128, 8], F32)
    dk = nc.alloc_sbuf_tensor("dk", [128, 8], F32)
    cdt = nc.alloc_sbuf_tensor("cdt", [128, 8], F32)
    maskt = nc.alloc_sbuf_tensor("maskt", [128, 4, 128], F32)
    smallt = nc.alloc_sbuf_tensor("smallt", [128, 4, 16], F32)
    ARENA_BYTES = 66 * 1024
    arena = nc.alloc_sbuf_tensor("arena", [128, ARENA_BYTES // 2], BF16)
    ps = nc.alloc_psum_tensor("ps", [128, 8, 512], F32)

    class Arena:
        def __init__(self):
            self.off = 0

        def reset(self):
            self.off = 0

        def bf(self, n):
            o = self.off
            self.off += ((2 * n + 31) // 32) * 32
            assert self.off <= ARENA_BYTES, self.off
            return arena[:, o // 2: o // 2 + n]

        def f32(self, n):
            o = self.off
            self.off += ((4 * n + 31) // 32) * 32
            assert self.off <= ARENA_BYTES, self.off
            return arena[:, o // 2: o // 2 + 2 * n].bitcast(F32)

    AR = Arena()

    b_xT = [[Buf(f"xT{k}_{t}") for t in range(3)] for k in range(8)]
    b_hT = [[Buf(f"hT{k}_{t}") for t in range(3)] for k in range(8)]
    b_ps = [Buf(f"ps{i}") for i in range(8)]
    for b in b_ps:
        b.excl = True
    b_sq = [Buf(f"sq{i}") for i in range(4)]
    b_rs = [Buf(f"rs{i}") for i in range(4)]
    b_tm = [Buf(f"tm{i}") for i in range(3)]
    b_const = S.dma_buf("const")
    b_PT = S.dma_buf("ptb")
    b_mod = [[Buf(f"mod{l}_{w}") for w in range(6)] for l in range(2)]
    b_gs = [[Buf(f"gs{l}_{w}") for w in range(2)] for l in range(2)]
    b_ret = Buf("rettab")
    b_small = [Buf(f"small{i}") for i in range(4)]
    sq_ring = Ring(range(4)); rs_ring = Ring([0]); nrs_ring = Ring([1, 2, 3]); tm_ring = Ring(range(3)); small_ring = Ring(range(4))
    bank_ring = Ring(range(8))
    slot_bufs = [S.dma_buf(f"slot{i}") for i in range(NSLOT)]

    def mm(out, lhsT, rhs, start, stop, r, w, inc=True):
        return S.op("pe", lambda: nc.tensor.matmul(out, lhsT, rhs, start=start, stop=stop), r, w, inc)

    def tr(out, in_, ident, r, w, inc=True):
        return S.op("pe", lambda: nc.tensor.transpose(out, in_, ident), r, w, inc)

    def act(out, in_, func, r, w, bias=None, scale=None):
        kw = {}
        if bias is not None:
            kw["bias"] = bias
        if scale is not None:
            kw["scale"] = scale
        return S.op("act", lambda: nc.scalar.activation(out=out, in_=in_, func=func, **kw), r, w)

    def tt(out, in0, in1, op, r, w, eng="dve"):
        e = nc.vector if eng == "dve" else nc.gpsimd
        return S.op(eng, lambda: e.tensor_tensor(out=out, in0=in0, in1=in1, op=op), r, w)

    def ts(out, in0, s1, s2, op0, op1, r, w, eng="dve"):
        e = nc.vector if eng == "dve" else nc.gpsimd
        if op1 is None:
            return S.op(eng, lambda: e.tensor_scalar(out=out, in0=in0, scalar1=s1, scalar2=None, op0=op0), r, w)
        return S.op(eng, lambda: e.tensor_scalar(out=out, in0=in0, scalar1=s1, scalar2=s2, op0=op0, op1=op1), r, w)

    def stt(out, in0, scalar, in1, op0, op1, r, w):
        return S.op("dve", lambda: nc.vector.scalar_tensor_tensor(out=out, in0=in0, scalar=scalar, in1=in1, op0=op0, op1=op1), r, w)

    def cp(eng, out, in_, r, w):
        if eng == "act":
            return act(out, in_, AF.Copy, r, w)
        e = nc.vector if eng == "dve" else nc.gpsimd
        return S.op(eng, lambda: e.tensor_copy(out, in_), r, w)

    ev_flip = [0]

    def evac(out, in_, r, w):
        ev_flip[0] ^= 1
        return cp("act" if ev_flip[0] else "dve", out, in_, r, w)

    stream = []
    stream_pos = [0]

    def mkblk(parts):
        b = Blk(parts)
        b.idx = len(stream)
        stream.append(b)
        return b

    done_ptr = [0]

    def prefetch():
        lim = min(len(stream), done_ptr[0] + NSLOT)
        while stream_pos[0] < lim:
            b = stream[stream_pos[0]]
            b.slot = stream_pos[0] % NSLOT
            b.buf = slot_bufs[b.slot]
            flat = wsl[:, b.slot, :]
            for (vf, src) in b.parts:
                S.dma("pool", vf(flat), src, writes=[b.buf], arena=False)
            stream_pos[0] += 1

    def ensure(blk):
        prefetch()
        assert blk.slot is not None, (blk.idx, done_ptr[0])

    def release(blk):
        blk.done = True
        while done_ptr[0] < len(stream) and getattr(stream[done_ptr[0]], "done", False):
            done_ptr[0] += 1
        prefetch()

    def v8(flat):
        return flat.rearrange("p (k n) -> p k n", k=8)

    def wblk_cols(src2d, c0, ncols=512):
        return mkblk([(lambda f, n=ncols: v8(f)[:, :, 0:n], src2d[:, c0:c0 + ncols].rearrange("(k p) n -> p k n", p=128))])

    b_id = Buf("ident")

    def setup():
        S.op("pool", lambda: nc.gpsimd.memset(identf[:], 1.0), [], [b_id])
        S.op("pool", lambda: nc.gpsimd.affine_select(out=identf[:], in_=identf[:], pattern=[[-1, 128]], compare_op=ALU.is_equal,
                                                     fill=0.0, base=0, channel_multiplier=1), [b_id], [b_id])
        S.op("pool", lambda: nc.gpsimd.memset(onesb[:], 1.0), [], [b_id])
        S.op("pool", lambda: nc.gpsimd.memset(epsb[:], EPS), [], [b_id])
        cp("dve", identb[:], identf[:], [b_id], [b_id])

    def setup_consts():
        for (dst, src) in [(cTt[:].rearrange("p k n -> p (k n)"), d_cT), (bmodt[:].rearrange("p l c -> p (l c)"), d_bmod),
                           (ngt[:].rearrange("p a b c -> p (a b c)"), d_ng), (fngt[:], d_fng), (qknt[:], d_qkn),
                           (dlt[:], d_dlog), (gnwt[:], d_gnw)]:
            S.dma("sp", dst, src[:, :], writes=[b_const])
        S.dma("sp", rtab[:], d_rtab[:, :], writes=[b_const])

    def setup_silu():
        act(scT[:], cTt[:], AF.Silu, [b_const], [b_const])

    def setup_pt():
        S.dma("pool", PTb[:], d_PT[:, :], writes=[b_PT], arena=False)
        S.dma("pool", PT2b[:], d_PT2[:, :], writes=[b_PT], arena=False)

    b_L = Buf("Ltab")
    b_rc = S.dma_buf("rconst")
    b_e = [Buf(f"escr{h}") for h in range(4)]

    def setup_ret_pieces():
        o = 41984
        f = lambda off, n: arena[:, (o + off) // 2:(o + off) // 2 + 2 * n].bitcast(F32)
        rcv = f(0, 641)
        Dm = rcv[:, 0:128]; tri_ge = rcv[:, 128:256]; tri_gt = rcv[:, 256:384]; rowi = rcv[:, 384:512]; rowj = rcv[:, 512:640]; pcol = rcv[:, 640:641]

        def pA():
            S.dma("sp", rcv, d_rconst[:, :], writes=[b_rc])
            act(Lt[:], dlt[:], AF.Exp, [b_const], [b_L], scale=-1.0)
            ts(Lt[:], Lt[:], 1.0, None, ALU.add, None, [b_L], [b_L])
            act(Lt[:], Lt[:], AF.Ln, [b_L], [b_L])
            ts(nLt[:], Lt[:], -1.0, None, ALU.mult, None, [b_L], [b_L])
            ts(pidx[:, 3:4], pcol, -127.0, None, ALU.add, None, [b_rc], [b_L])
            ts(pidx[:, 1:2], pcol, -1.0, None, ALU.mult, None, [b_rc], [b_L])
            act(dk[:, 0:4], Lt[:, 0:4], AF.Exp, [b_L], [b_ret], scale=pidx[:, 3:4])
            act(dk[:, 4:8], Lt[:, 4:8], AF.Exp, [b_L], [b_ret], scale=pidx[:, 1:2])
            act(cdt[:], Lt[:], AF.Exp, [b_L], [b_ret], scale=-128.0)

        def pR():
            for h in range(4):
                act(DFrow[:, h, :], rowi, AF.Exp, [b_L, b_rc], [b_ret], scale=nLt[:, h:h + 1])
                act(DBrow[:, h, :], rowj, AF.Exp, [b_L, b_rc], [b_ret], scale=nLt[:, 4 + h:5 + h])
            ts(DFrow[:], DFrow[:], 0.0625, None, ALU.mult, None, [b_ret], [b_ret])
            ts(DBrow[:], DBrow[:], 0.0625, None, ALU.mult, None, [b_ret], [b_ret])

        def pM(h):
            def g():
                e = f(2624 + h * 1536, 384).rearrange("p (a n) -> p a n", a=3)
                act(e[:, 0, :], Dm, AF.Exp, [b_L, b_rc], [b_e[h]], scale=nLt[:, h:h + 1])
                act(e[:, 1, :], Dm, AF.Exp, [b_L, b_rc], [b_e[h]], scale=Lt[:, 4 + h:5 + h])
                act(e[:, 2, :], rowi, AF.Exp, [b_L, b_rc], [b_e[h]], scale=Lt[:, h:h + 1])
                tt(e[:, 0, :], e[:, 0, :], tri_ge, ALU.mult, [b_e[h], b_rc], [b_e[h]])
                tt(e[:, 1, :], e[:, 1, :], tri_gt, ALU.mult, [b_e[h], b_rc], [b_e[h]])
                tt(e[:, 0, :], e[:, 0, :], e[:, 1, :], ALU.add, [b_e[h]], [b_e[h]])
                tt(maskt[:, h, :], e[:, 0, :], e[:, 2, :], ALU.mult, [b_e[h]], [b_ret])
            return g
        return [pA, pR, pM(0), pM(1), pM(2), pM(3)]

    LI = {}

    def li_alloc():
        AR.reset()
        LI["xin"] = AR.f32(2 * 4 * 1024).rearrange("p (r j d) -> p r j d", r=2, j=4)
        LI["b"] = [S.dma_buf(f"xin{i}") for i in range(2)]

    def li_dma(t):
        def f():
            src = d_xs[t * 512:(t + 1) * 512, :] if t < 2 else d_xp[:, :]
            r = t % 2
            S.dma("sp", LI["xin"][:, r], src.rearrange("(j p) d -> p j d", p=128), writes=[LI["b"][r]])
        return f

    def li_tr(t):
        def f():
            r = t % 2
            xin = LI["xin"]
            for kc in range(8):
                bk = bank_ring.next()
                for j in range(4):
                    tr(ps[:, bk, j * 128:(j + 1) * 128], xin[:, r, j, kc * 128:(kc + 1) * 128], identf[:],
                       [LI["b"][r], b_id], [b_ps[bk]], inc=(j == 3))
                evac(xT[:, kc, t * 512:(t + 1) * 512], ps[:, bk, :], [b_ps[bk]], [b_xT[kc][t]])
        return f

    tiny_q = []

    def drain():
        while tiny_q:
            tiny_q.pop(0)()

    def mod_blk(l, hb, bank=None, spread=False):
        blk = wblk_cols(d_wmod[l], hb * 512)

        def emit():
            ensure(blk)
            sv = v8(wsl[:, blk.slot, :])
            bk = 7 if spread else (bank_ring.next() if bank is None else bank)
            ops = []
            for oc in range(4):
                for kc in range(8):
                    ops.append(lambda oc=oc, kc=kc: mm(ps[:, bk, oc * 2:oc * 2 + 2], sv[:, kc, oc * 128:(oc + 1) * 128], scT[:, kc, :],
                                                       kc == 0, kc == 7, [blk.buf, b_const], [b_ps[bk]], inc=(kc == 7)))

            def fin():
                w = hb // 2
                c0 = hb * 4
                pv = ps[:, bk, 0:8].rearrange("p (c n) -> p c n", n=2)
                for n in range(2):
                    tt(modT[:, l, c0:c0 + 4, n], pv[:, :, n], bmodt[:, l, c0:c0 + 4], ALU.add, [b_ps[bk], b_const], [b_mod[l][w]])
                release(blk)
                if hb in (3, 9):
                    wn = 0 if hb == 3 else 1
                    wsc = 1 if hb == 3 else 4
                    for n in range(2):
                        stt(gsT[:, l, wn, :, n], modT[:, l, wsc * 8:wsc * 8 + 8, n], 1.0, ngt[:, l, wn, :], ALU.add, ALU.mult,
                            [b_mod[l][wsc], b_const], [b_gs[l][wn]])
            if spread:
                tiny_q.extend(ops)
                tiny_q.append(fin)
            else:
                for o_ in ops:
                    o_()
                fin()
        return emit

    def norm_items(l, wn):
        wsh = 0 if wn == 0 else 3
        items = []
        for t in range(3):
            n = 1 if t < 2 else 0
            sl = slice(t * 512, (t + 1) * 512)
            st = {}

            def s0(t=t, sl=sl, st=st):
                st["sq"] = []
                for kc in range(8):
                    u = sq_ring.next()
                    st["sq"].append(u)
                    if kc % 2 == 0:
                        act(sqr[:, u, :], xT[:, kc, sl], AF.Square, [b_xT[kc][t]], [b_sq[u]])
                    else:
                        tt(sqr[:, u, :], xT[:, kc, sl], xT[:, kc, sl], ALU.mult, [b_xT[kc][t]], [b_sq[u]])
                    if kc == 0:
                        st["bk"] = bank_ring.next()
                    bk = st["bk"]
                    mm(ps[:, bk, :], onesb[:], sqr[:, u, :], kc == 0, kc == 7, [b_sq[u], b_id], [b_ps[bk]], inc=True)
                bk = st["bk"]
                ru = nrs_ring.next()
                st["ru"] = ru
                act(rsr[:, ru, :], ps[:, bk, :], AF.Ln, [b_ps[bk], b_id], [b_rs[ru]], bias=epsb[:, 0:1], scale=1.0 / D)
                act(rsr[:, ru, :], rsr[:, ru, :], AF.Exp, [b_rs[ru]], [b_rs[ru]], scale=-0.5)

            def s1(t=t, sl=sl, st=st, n=n):
                ru = st["ru"]
                for kc in range(8):
                    tu = tm_ring.next()
                    tt(tmr[:, tu, :], xT[:, kc, sl], rsr[:, ru, :], ALU.mult, [b_xT[kc][t], b_rs[ru]], [b_tm[tu]])
                    act(hT[:, kc, sl], tmr[:, tu, :], AF.Identity, [b_tm[tu], b_gs[l][wn], b_mod[l][wsh]], [b_hT[kc][t]],
                        bias=modT[:, l, wsh * 8 + kc, n:n + 1], scale=gsT[:, l, wn, kc, n:n + 1])
            items.append([s0, s1])
        return items

    def norm_phase(l, wn):
        return run_pipeline(norm_items(l, wn), [0, 1])

    def norm_hooks(l, wn, order=(0, 1, 2)):
        it = norm_items(l, wn)
        o0, o1, o2 = order

        def after_tile(t):
            if t == o0:
                it[o0][0]()
            elif t == o1:
                it[o1][0](); it[o0][1]()
            elif t == o2:
                it[o2][0](); it[o1][1]()

        def tail():
            it[o2][1]()
        return after_tile, tail

    def proj_ws(blk, nchunk, kcn, slot_view, rhs_fn, rhs_bufs_fn, evac_fn, tiles=(0, 1, 2), after_tile=None):
        ensure(blk)
        sv = slot_view(wsl[:, blk.slot, :])

        def grp(oc, t):
            bk = bank_ring.next()
            for kc in range(kcn):
                mm(ps[:, bk, :], sv[:, kc, oc * 128:(oc + 1) * 128], rhs_fn(kc, t), kc == 0, kc == kcn - 1,
                   [blk.buf] + rhs_bufs_fn(kc, t), [b_ps[bk]], inc=(kc == kcn - 1))
                if tiny_q and kc < kcn - 1:
                    tiny_q.pop(0)()
            evac_fn(oc, t, bk)
        if after_tile is None:
            for oc in range(nchunk):
                for t in tiles:
                    grp(oc, t)
        else:
            pending = None
            for t in tiles:
                for oc in range(nchunk):
                    grp(oc, t)
                    if pending is not None and oc == 1:
                        after_tile(pending)
                        pending = None
                pending = t
            after_tile(pending)
        release(blk)

    def tsl(t):
        return slice(t * 512, (t + 1) * 512)

    A = {}

    def attn_alloc():
        AR.reset()
        A["qT"] = AR.bf(8 * NT).rearrange("p (h n) -> p h n", h=8)
        A["kT"] = AR.bf(2 * 2048).rearrange("p (h n) -> p h n", h=2)
        A["V"] = AR.bf(16 * 256).rearrange("p (c n) -> p c n", c=16)
        A["P"] = AR.bf(8 * 512)
        A["cos"] = AR.f32(1024); A["sin"] = AR.f32(1024)
        A["qn"] = AR.bf(2 * 512).rearrange("p (r n) -> p r n", r=2)
        A["scr"] = AR.f32(2048)
        A["b_qT"] = [[Buf(f"qT{h}_{t}") for t in range(3)] for h in range(8)]
        A["b_kT"] = [[Buf(f"kT{h}_{r}") for r in range(4)] for h in range(2)]
        A["b_V"] = [Buf(f"V{r}") for r in range(4)]
        A["b_P"] = [Buf(f"P{u}") for u in range(8)]
        A["b_tab"] = S.dma_buf("atab")
        A["b_qn"] = [Buf(f"qn{i}") for i in range(2)]
        A["b_scr"] = [S.dma_buf(f"scr{i}") for i in range(4)]
        A["b_ckb"] = S.dma_buf("ckb")
        A["b_cvb"] = S.dma_buf("cvb")
        A["P_ring"] = Ring(range(8)); A["qn_ring"] = Ring(range(2)); A["t1_ring"] = Ring([0, 1])
        A["main_ring"] = Ring([0, 1, 2, 3]); A["aux_ring"] = Ring([4, 5, 6, 7])
        rq2 = A["P"][:, 6 * 512:8 * 512].bitcast(F32)
        A["rq_ring"] = Ring([(rsr[:, 0, :], [b_rs[0]]), (rq2, [A["b_P"][6], A["b_P"][7]])])
        S.dma("sp", A["cos"], d_cosA[:, :], writes=[A["b_tab"]])
        S.dma("sp", A["sin"], d_sinA[:, :], writes=[A["b_tab"]])

    def attn_cache():
        ckb = A["P"][:, 4 * 512:6 * 512].rearrange("p (c n) -> p c n", c=4)
        bP = A["b_P"]
        S.dma("pool", ckb, d_ck.rearrange("(c p) n -> p c n", p=128), writes=[bP[4], bP[5]], chan=A["b_ckb"])
        S.dma("pool", A["V"][:, 0:4, :], d_cv.rearrange("(c p) n -> p c n", p=128), writes=[A["b_V"][0]], chan=A["b_cvb"])
        for kvh in range(2):
            bk = bank_ring.next()
            pb = ps[:, bk, :].bitcast(BF16)
            for c in range(4):
                tr(pb[:, c * 128:(c + 1) * 128], ckb[:, c, kvh * 128:(kvh + 1) * 128], identb[:], [bP[4], bP[5], b_id], [b_ps[bk]], inc=(c == 3))
            evac(A["kT"][:, kvh, 0:512], pb[:, 0:512], [b_ps[bk]], [A["b_kT"][kvh][0]])

    def qk_items(blk, heads, is_k):
        items = []
        gcol = 1 if is_k else 0
        for (oc, hd) in heads:
            for t in range(3):
                st = {}
                sl = tsl(t)

                def s0(oc=oc, t=t, st=st, sl=sl):
                    ensure(blk)
                    sv = v8(wsl[:, blk.slot, :])
                    bk = A["main_ring"].next()
                    st["bk"] = bk
                    for kc in range(8):
                        mm(ps[:, bk, :], sv[:, kc, oc * 128:(oc + 1) * 128], hT[:, kc, sl], kc == 0, kc == 7,
                           [blk.buf, b_hT[kc][t]], [b_ps[bk]], inc=(kc == 7))
                    u = sq_ring.next()
                    st["sq"] = u
                    act(sqr[:, u, :], ps[:, bk, :], AF.Square, [b_ps[bk]], [b_sq[u]])

                def s1(st=st):
                    u = st["sq"]
                    b2 = A["aux_ring"].next()
                    mm(ps[:, b2, :], onesb[:], sqr[:, u, :], True, True, [b_sq[u], b_id], [b_ps[b2]])
                    rq, rqb = A["rq_ring"].next()
                    st["rq"] = (rq, rqb)
                    act(rq, ps[:, b2, :], AF.Ln, [b_ps[b2], b_id], rqb, bias=epsb[:, 0:1], scale=1.0 / 128)
                    act(rq, rq, AF.Exp, rqb, rqb, scale=-0.5)

                def s2(hd=hd, t=t, st=st, sl=sl):
                    bk = st["bk"]
                    rq, rqb = st["rq"]
                    if t < 2:
                        qi = A["qn_ring"].next()
                        st["qn"] = qi
                        stt(A["qn"][:, qi, :], ps[:, bk, :], qknt[:, gcol:gcol + 1], rq, ALU.mult, ALU.mult,
                            [b_ps[bk], b_const] + rqb, [A["b_qn"][qi]])
                    elif not is_k:
                        stt(A["qT"][:, hd, sl], ps[:, bk, :], qknt[:, gcol:gcol + 1], rq, ALU.mult, ALU.mult,
                            [b_ps[bk], b_const] + rqb, [A["b_qT"][hd][2]])
                    else:
                        bP = A["b_P"]
                        kn32 = A["P"][:, hd * 1024:(hd + 1) * 1024].bitcast(F32)
                        ub = [bP[2 * hd], bP[2 * hd + 1]]
                        stt(kn32, ps[:, bk, :], qknt[:, 1:2], rq, ALU.mult, ALU.mult, [b_ps[bk], b_const] + rqb, ub)
                        cp("act", A["kT"][:, hd, 1536:2048], kn32, ub, [A["b_kT"][hd][3]])
                        b3 = A["aux_ring"].next()
                        for j in range(4):
                            tr(ps[:, b3, j * 128:(j + 1) * 128], kn32[:, j * 128:(j + 1) * 128], identf[:], ub + [b_id], [b_ps[b3]], inc=(j == 3))
                        nkst = A["scr"][:, 0:1024].rearrange("p (j c) -> p j c", j=4)
                        cp("dve", nkst[:, :, hd * 128:(hd + 1) * 128], ps[:, b3, :].rearrange("p (j c) -> p j c", j=4), [b_ps[b3]],
                           [A["b_scr"][0], A["b_scr"][1]])
                        if hd == 1:
                            S.dma("sp", o_nk.rearrange("(j p) c -> p j c", p=128), nkst, reads=[A["b_scr"][0], A["b_scr"][1]], chan=A["b_scr"][0])

                def s3(t=t, st=st, sl=sl):
                    if t >= 2:
                        return
                    qi = st["qn"]
                    b3 = A["aux_ring"].next()
                    st["b3"] = b3
                    mm(ps[:, b3, :], PTb[:], A["qn"][:, qi, :], True, True, [A["b_qn"][qi], b_PT], [b_ps[b3]])
                    t1 = A["t1_ring"].next()
                    st["t1"] = t1
                    tt(tmr[:, t1, :], A["qn"][:, qi, :], A["cos"][:, sl], ALU.mult, [A["b_qn"][qi], A["b_tab"]], [b_tm[t1]])

                def s4(hd=hd, t=t, st=st, sl=sl):
                    if t >= 2:
                        return
                    b3 = st["b3"]; t1 = st["t1"]
                    t2 = 2
                    tt(tmr[:, t2, :], ps[:, b3, :], A["sin"][:, sl], ALU.mult, [b_ps[b3], A["b_tab"]], [b_tm[t2]])
                    if is_k:
                        dst = A["kT"][:, hd, 512 + t * 512:512 + (t + 1) * 512]; db = A["b_kT"][hd][1 + t]
                    else:
                        dst = A["qT"][:, hd, sl]; db = A["b_qT"][hd][t]
                    tt(dst, tmr[:, t1, :], tmr[:, t2, :], ALU.add, [b_tm[t1], b_tm[t2]], [db])
                items.append([s0, s1, s2, s3, s4, t])
        return items

    def v_proj(blk):
        ensure(blk)
        sv = v8(wsl[:, blk.slot, :])
        nvst = A["scr"][:, 1024:2048].rearrange("p (j c) -> p j c", j=4)
        for tc in range(12):
            t = tc // 4
            bk = bank_ring.next()
            for kc in range(8):
                mm(ps[:, bk, 0:256], hT[:, kc, tc * 128:(tc + 1) * 128], sv[:, kc, 256:512], kc == 0, kc == 7,
                   [blk.buf, b_hT[kc][t]], [b_ps[bk]], inc=(kc == 7))
            if tc < 8:
                evac(A["V"][:, 4 + tc, :], ps[:, bk, 0:256], [b_ps[bk]], [A["b_V"][1 + t]])
            else:
                cp("dve", nvst[:, tc - 8, :], ps[:, bk, 0:256], [b_ps[bk]], [A["b_scr"][2], A["b_scr"][3]])
                cp("act", A["V"][:, 4 + tc, :], nvst[:, tc - 8, :], [A["b_scr"][2], A["b_scr"][3]], [A["b_V"][3]])
        if True:
            S.dma("sp", o_nv.rearrange("(j p) c -> p j c", p=128), nvst, reads=[A["b_scr"][2], A["b_scr"][3]], chan=A["b_scr"][2])

    def attention_steps():
        items = []
        for t in range(2):
            for h in range(8):
                items.append(dict(h=h, t=t, q0=t * 512, qn=512, kcs=list(range(12))))
        for pbi in range(2):
            for h in range(0, 8, 2):
                items.append(dict(h=h, t=2, q0=1024 + pbi * 256, qn=512, kcs=[12 + 2 * pbi, 13 + 2 * pbi], pair=True))
        s_ring = Ring([0, 1, 2])
        acc_ring = Ring([(3, 5), (4, 6)])
        micro = []
        for it in items:
            it["st"] = {}
            for j, kc in enumerate(it["kcs"]):
                micro.append((it, j, kc))

        def kbuf(kvh, kc):
            if kc < 4:
                return A["b_kT"][kvh][0]
            if kc < 12:
                return A["b_kT"][kvh][1 + (kc - 4) // 4]
            return A["b_kT"][kvh][3]

        def vbuf(kc):
            if kc < 4:
                return A["b_V"][0]
            if kc < 12:
                return A["b_V"][1 + (kc - 4) // 4]
            return A["b_V"][3]

        def qview(it):
            h = it["h"]; q0 = it["q0"]
            if it.get("pair"):
                return A["qT"][:, h:h + 2, q0:q0 + 256], [A["b_qT"][h][2], A["b_qT"][h + 1][2]]
            return A["qT"][:, h, q0:q0 + it["qn"]], [A["b_qT"][h][it["t"]]]

        def pv2(ap, it):
            return ap.rearrange("p (a q) -> p a q", a=2) if it.get("pair") else ap

        def S_stage(it, j, kc):
            def f():
                h = it["h"]; kvh = h // 4; qn = it["qn"]
                bk = s_ring.next()
                qv, qb = qview(it)
                mm(pv2(ps[:, bk, 0:qn], it), A["kT"][:, kvh, kc * 128:(kc + 1) * 128], qv, True, True,
                   [kbuf(kvh, kc)] + qb, [b_ps[bk]])
                if tiny_q:
                    tiny_q.pop(0)()
                u = A["P_ring"].next()
                it["st"][j] = u
                act(A["P"][:, u * 512:u * 512 + qn], ps[:, bk, 0:qn], AF.Exp, [b_ps[bk]], [A["b_P"][u]], scale=128.0 ** -0.5)
            return f

        def PV_stage(it, j, kc):
            def f():
                h = it["h"]; kvh = h // 4; qn = it["qn"]; q0 = it["q0"]
                nk_ = len(it["kcs"])
                if j == 0:
                    it["acc"] = acc_ring.next()
                ob, dbk = it["acc"]
                u = it["st"][j]
                P = A["P"][:, u * 512:u * 512 + qn]
                mm(ps[:, ob, 0:qn], A["V"][:, kc, kvh * 128:(kvh + 1) * 128], P, j == 0, j == nk_ - 1, [vbuf(kc), A["b_P"][u]], [b_ps[ob]],
                   inc=(j == nk_ - 1))
                if tiny_q:
                    tiny_q.pop(0)()
                mm(ps[:, dbk, 0:qn], onesb[:], P, j == 0, j == nk_ - 1, [A["b_P"][u], b_id], [b_ps[dbk]], inc=True)
                if j == nk_ - 1:
                    ru = rs_ring.next()
                    if it["t"] == 2:
                        act(rsr[:, ru, 0:qn], ps[:, dbk, 0:qn], AF.Ln, [b_ps[dbk]], [b_rs[ru]])
                        act(rsr[:, ru, 0:qn], rsr[:, ru, 0:qn], AF.Exp, [b_rs[ru]], [b_rs[ru]], scale=-1.0)
                    else:
                        S.op("dve", lambda: nc.vector.reciprocal(rsr[:, ru, 0:qn], ps[:, dbk, 0:qn]), [b_ps[dbk]], [b_rs[ru]])
                    if it.get("pair"):
                        dst = hT[:, h:h + 2, q0:q0 + 256]; db = [b_hT[h][2], b_hT[h + 1][2]]
                    else:
                        dst = hT[:, h, q0:q0 + qn]; db = [b_hT[h][it["t"]]]
                    tt(dst, pv2(ps[:, ob, 0:qn], it), pv2(rsr[:, ru, 0:qn], it), ALU.mult, [b_ps[ob], b_rs[ru]], db)
            return f

        LA = 2
        out = []
        n = len(micro)
        for i in range(n + LA):
            if i < n:
                out.append(S_stage(*micro[i]))
            if i >= LA:
                out.append(PV_stage(*micro[i - LA]))
        return out

    def resid_evac(l, wg, oc_base):
        def f(oc, t, bk):
            n = 1 if t < 2 else 0
            o = oc_base + oc
            stt(xT[:, o, tsl(t)], ps[:, bk, :], modT[:, l, wg * 8 + o, n:n + 1], xT[:, o, tsl(t)], ALU.mult, ALU.add,
                [b_ps[bk], b_mod[l][wg], b_xT[o][t]], [b_xT[o][t]])
        return f

    M = {}

    def mlp_alloc():
        AR.reset()
        bank_ring.items = list(range(7))
        M["h1"] = AR.bf(8 * 4 * 512).rearrange("p (u k n) -> p u k n", u=8, k=4)
        M["rt"] = AR.f32(4 * 512).rearrange("p (r n) -> p r n", r=4)
        M["b_h1"] = [Buf(f"h1_{u}") for u in range(8)]
        M["b_rt"] = [Buf(f"rt{u}") for u in range(4)]
        M["h1_ring"] = Ring(range(8)); M["rt_ring"] = Ring(range(4))

    def mlp_steps(l, extras, last_hook=None, pre_extra=None, last_order=(0, 1, 2)):
        steps = []
        units = {}

        def w1_step(g):
            blk = wblk_cols(d_w1[l], g * 512)

            def f():
                us = [M["h1_ring"].next() for _ in range(3)]
                units[g] = us

                def ev(oc, t, bk):
                    r = M["rt_ring"].next()
                    act(M["rt"][:, r, :], ps[:, bk, :], AF.Relu, [b_ps[bk]], [M["b_rt"][r]])
                    tt(M["h1"][:, us[t], oc, :], M["rt"][:, r, :], M["rt"][:, r, :], ALU.mult, [M["b_rt"][r]], [M["b_h1"][us[t]]])
                proj_ws(blk, 4, 8, v8, lambda kc, t: hT[:, kc, tsl(t)], lambda kc, t: [b_hT[kc][t]], ev,
                        after_tile=((lambda t: None) if g == 0 else None))
            return blk, f

        def w2_step(g):
            blk = mkblk([(lambda f: f.rearrange("p (k n) -> p k n", k=4), d_w2[l][g * 512:(g + 1) * 512, :].rearrange("(k p) n -> p k n", p=128))])

            def f():
                us = units[g]
                proj_ws(blk, 8, 4, lambda fl: fl.rearrange("p (k n) -> p k n", k=4), lambda kc, t: M["h1"][:, us[t], kc, :],
                        lambda kc, t: [M["b_h1"][us[t]]], resid_evac(l, 5, 0), after_tile=(last_hook if g == 7 else None),
                        tiles=(last_order if g == 7 else (0, 1, 2)))
            return blk, f
        ex = list(extras)

        def pop_extra():
            if ex:
                grp_ = ex.pop(0)
                for e_ in (grp_ if isinstance(grp_, list) else [grp_]):
                    if callable(e_):
                        steps.append(e_)
                    else:
                        steps.append(mod_blk(e_[0], e_[1], spread=True))
        steps.append(w1_step(0)[1]); pop_extra(); pop_extra()
        steps.append(w1_step(1)[1])
        steps.append(drain)
        for pe_ in (pre_extra or []):
            steps.append(pe_)
        for g in range(8):
            if g == 7:
                steps.append(drain)
            steps.append(w2_step(g)[1])
            if g + 2 < 8:
                steps.append(w1_step(g + 2)[1])
            if g >= 1:
                pop_extra()
        while ex:
            pop_extra()
        steps.append(drain)
        return steps

    def wblk_placeholder():
        return None

    R = {}

    def ret_alloc():
        AR.reset()
        R["qfb"] = AR.bf(2 * 2 * 1024).rearrange("p (v k n) -> p v k n", v=2, k=2)
        R["kT"] = AR.bf(2 * 1024).rearrange("p (k n) -> p k n", k=2)
        R["v"] = AR.bf(8 * 256).rearrange("p (c n) -> p c n", c=8)
        R["kfb"] = AR.bf(8 * 2 * 256).rearrange("p (c v n) -> p c v n", c=8, v=2)
        R["sgT"] = AR.bf(2 * 1024).rearrange("p (k n) -> p k n", k=2)
        R["gT"] = AR.bf(2 * NT).rearrange("p (k n) -> p k n", k=2)
        R["Sst"] = AR.bf(8 * 512).rearrange("p (c k n) -> p c k n", c=8, k=2)
        R["Sfr"] = AR.bf(3 * 512).rearrange("p (r k n) -> p r k n", r=3, k=2)
        R["Sm"] = AR.f32(2 * 512).rearrange("p (d k n) -> p d k n", d=2, k=2)
        R["vT"] = AR.bf(2 * 512).rearrange("p (r n) -> p r n", r=2)
        R["Sm2"] = AR.f32(512).rearrange("p (k n) -> p k n", k=2)
        R["qraw"] = AR.bf(3 * 512).rearrange("p (r n) -> p r n", r=3)
        R["rt"] = AR.f32(4 * 512).rearrange("p (r n) -> p r n", r=4)
        R["sc"] = AR.bf(2 * 128).rearrange("p (r n) -> p r n", r=2)
        R["on"] = AR.bf(2 * 256).rearrange("p (r n) -> p r n", r=2)
        nb = lambda nm, n: [Buf(f"{nm}{i}") for i in range(n)]
        R["b_qfb"] = [nb("qfb0_", 2), nb("qfb1_", 2)]
        R["b_kT"] = [nb("rkT0_", 2), nb("rkT1_", 2)]
        R["b_v"] = nb("rv", 8); R["b_kfb"] = nb("kfb", 8)
        R["b_sgT"] = [nb("sgT0_", 2), nb("sgT1_", 2)]
        R["b_gT"] = [[Buf(f"gT{k}_{t}") for t in range(3)] for k in range(2)]
        R["b_Sst"] = nb("Sst", 8); R["b_Sfr"] = nb("Sfr", 3)
        R["b_Sm"] = [S.dma_buf("Sm0"), S.dma_buf("Sm1")]
        R["b_vT"] = nb("vT", 2); R["b_Sm2"] = S.dma_buf("Sm2"); R["b_qraw"] = nb("qraw", 3); R["b_rt"] = nb("rt", 4)
        R["b_sc"] = nb("sc", 2); R["b_on"] = nb("on", 2)
        R["Sfr_ring"] = Ring(range(3)); R["vT_ring"] = Ring(range(2)); R["qraw_ring"] = Ring(range(3)); R["rt_ring"] = Ring(range(4))
        R["sc_ring"] = Ring(range(2)); R["on_ring"] = Ring(range(2)); R["o_ring"] = Ring([6, 7])
        bank_ring.items = list(range(6))

    def rope_tabs(dkc, t):
        if dkc == 0:
            c = rtab[:, 8 * t:8 * t + 8].unsqueeze(2).broadcast_to([128, 8, 64])
            s_ = rtab[:, 16 + 8 * t:16 + 8 * t + 8].unsqueeze(2).broadcast_to([128, 8, 64])
        else:
            c = rtab[:, 32:96].unsqueeze(1).broadcast_to([128, 8, 64])
            s_ = rtab[:, 96:160].unsqueeze(1).broadcast_to([128, 8, 64])
        return c, s_

    def ret_head_steps(h, wo_hook=None):
        steps = []
        blkA = mkblk([(lambda f: v8(f)[:, :, 0:256], d_wr[:, h * 256:(h + 1) * 256].rearrange("(k p) n -> p k n", p=128)),
                      (lambda f: v8(f)[:, :, 256:512], d_wr[:, 1024 + h * 256:1024 + (h + 1) * 256].rearrange("(k p) n -> p k n", p=128))])
        blkB = mkblk([(lambda f: v8(f)[:, :, 0:256], d_wr[:, 2048 + h * 256:2048 + (h + 1) * 256].rearrange("(k p) n -> p k n", p=128)),
                      (lambda f: v8(f)[:, :, 256:512], d_wr[:, 3072 + h * 256:3072 + (h + 1) * 256].rearrange("(k p) n -> p k n", p=128))])
        blkO = mkblk([(lambda f: f[:, 0:2048].rearrange("p (k n) -> p k n", k=2), d_wro[h * 256:(h + 1) * 256, :].rearrange("(k p) n -> p k n", p=128))])

        def r8(ap):
            return ap.rearrange("p (r c) -> p r c", r=8)

        def r4(ap):
            return ap.rearrange("p (j n) -> p j n", j=4)

        def qk_item(kind, dkc, lt, t, rope):
            oc = dkc if kind == "q" else 2 + dkc
            st = {}
            lsl = slice(lt * 512, (lt + 1) * 512)

            def s0():
                ensure(blkA)
                sv = v8(wsl[:, blkA.slot, :])
                bk = bank_ring.next()
                for kc in range(8):
                    mm(ps[:, bk, :], sv[:, kc, oc * 128:(oc + 1) * 128], hT[:, kc, tsl(t)], kc == 0, kc == 7, [blkA.buf, b_hT[kc][t]], [b_ps[bk]],
                       inc=(kc == 7))
                if rope:
                    r = R["qraw_ring"].next(); st["qraw"] = r
                    cp("act", R["qraw"][:, r, :], ps[:, bk, :], [b_ps[bk]], [R["b_qraw"][r]])
                else:
                    u = R["rt_ring"].next(); st["qr"] = u
                    cp("act", R["rt"][:, u, :], ps[:, bk, :], [b_ps[bk]], [R["b_rt"][u]])

            def s1():
                if rope:
                    r = st["qraw"]
                    b2 = bank_ring.next()
                    mm(ps[:, b2, :], PT2b[:], R["qraw"][:, r, :], True, True, [R["b_qraw"][r], b_PT], [b_ps[b2]])
                    cosv, sinv = rope_tabs(dkc, t)
                    u1 = R["rt_ring"].next(); u2 = R["rt_ring"].next()
                    tt(r8(R["rt"][:, u1, :]), r8(R["qraw"][:, r, :]), cosv, ALU.mult, [R["b_qraw"][r], b_const], [R["b_rt"][u1]])
                    tt(r8(R["rt"][:, u2, :]), r8(ps[:, b2, :]), sinv, ALU.mult, [b_ps[b2], b_const], [R["b_rt"][u2]])
                    tt(R["rt"][:, u1, :], R["rt"][:, u1, :], R["rt"][:, u2, :], ALU.add, [R["b_rt"][u1], R["b_rt"][u2]], [R["b_rt"][u1]])
                    u = u1
                else:
                    u = st["qr"]
                qr = R["rt"][:, u, :]
                if kind == "q":
                    for v_, tab in ((0, DFrow), (1, DBrow)):
                        tt(r4(R["qfb"][:, v_, dkc, lsl]), r4(qr), tab[:, h, :].unsqueeze(1).broadcast_to([128, 4, 128]), ALU.mult,
                           [R["b_rt"][u], b_ret], [R["b_qfb"][dkc][lt]])
                else:
                    cp("act", R["kT"][:, dkc, lsl], qr, [R["b_rt"][u]], [R["b_kT"][dkc][lt]])

            def s2():
                if kind != "k":
                    return
                bk = bank_ring.next()
                pb = ps[:, bk, :].bitcast(BF16)
                for j in range(4):
                    tr(pb[:, j * 128:(j + 1) * 128], R["kT"][:, dkc, lt * 512 + j * 128:lt * 512 + (j + 1) * 128], identb[:],
                       [R["b_kT"][dkc][lt], b_id], [b_ps[bk]], inc=(j == 3))
                c0 = lt * 4
                cb = [R["b_kfb"][c0 + j] for j in range(4)]
                src = pb[:, 0:512].rearrange("p (j n) -> p j n", j=4)
                act(R["kfb"][:, c0:c0 + 4, 0, dkc * 128:(dkc + 1) * 128], src, AF.Identity, [b_ps[bk], b_ret], cb, scale=dk[:, h:h + 1])
                ts(R["kfb"][:, c0:c0 + 4, 1, dkc * 128:(dkc + 1) * 128], src, dk[:, 4 + h:5 + h], None, ALU.mult, None, [b_ps[bk], b_ret], cb)
            return [s0, s1, s2]

        def vg_item(kind, dvc, lt, t):
            oc = dvc if kind == "v" else 2 + dvc
            st = {}
            lsl = slice(lt * 512, (lt + 1) * 512)

            def s0():
                ensure(blkB)
                sv = v8(wsl[:, blkB.slot, :])
                bk = bank_ring.next()
                for kc in range(8):
                    mm(ps[:, bk, :], sv[:, kc, oc * 128:(oc + 1) * 128], hT[:, kc, tsl(t)], kc == 0, kc == 7, [blkB.buf, b_hT[kc][t]], [b_ps[bk]],
                       inc=(kc == 7))
                if kind == "v":
                    r = R["vT_ring"].next(); st["vT"] = r
                    evac(R["vT"][:, r, :], ps[:, bk, :], [b_ps[bk]], [R["b_vT"][r]])
                else:
                    act(R["sgT"][:, dvc, lsl], ps[:, bk, :], AF.Silu, [b_ps[bk]], [R["b_sgT"][dvc][lt]])

            def s1():
                if kind != "v":
                    return
                r = st["vT"]
                bk = bank_ring.next()
                pb = ps[:, bk, :].bitcast(BF16)
                for j in range(4):
                    tr(pb[:, j * 128:(j + 1) * 128], R["vT"][:, r, j * 128:(j + 1) * 128], identb[:], [R["b_vT"][r], b_id], [b_ps[bk]], inc=(j == 3))
                c0 = lt * 4
                evac(R["v"][:, c0:c0 + 4, dvc * 128:(dvc + 1) * 128], pb[:, 0:512].rearrange("p (j n) -> p j n", j=4), [b_ps[bk]],
                     [R["b_v"][c0 + j] for j in range(4)])
            return [s0, s1, None]

        def state_init(d, sq):
            def f():
                if sq["pb"] is None:
                    S.dma("sp", R["Sm"][:, d], d_sr[d, h].rearrange("(k p) n -> p k n", p=128), writes=[R["b_Sm"][d]])
                else:
                    S.op("dve", lambda: nc.vector.memset(R["Sm"][:, d], 0.0), [], [R["b_Sm"][d]])
            return f

        def state_update(d, lc, last, sq, src=None, dst=None):
            if last and sq["pb"] is None:
                return
            if src is None:
                src = dst = (R["Sm"][:, d], R["b_Sm"][d])
            bk = bank_ring.next()
            pv = ps[:, bk, :].rearrange("p (k n) -> p k n", k=2)
            for dkc in range(2):
                mm(pv[:, dkc, :], R["kfb"][:, lc, d, dkc * 128:(dkc + 1) * 128], R["v"][:, lc, :], True, True,
                   [R["b_kfb"][lc], R["b_v"][lc]], [b_ps[bk]], inc=(dkc == 1))
            stt(dst[0], src[0], cdt[:, 4 * d + h:4 * d + h + 1], pv, ALU.mult, ALU.add,
                [src[1], b_ps[bk], b_ret], [dst[1]])
            if last:
                S.dma("sp", o_ns[sq["pb"], d, h].rearrange("(k p) n -> p k n", p=128), dst[0], reads=[dst[1]])

        def bwd_steps(sq):
            out = [state_init(1, sq)]
            nch = sq["nch"]
            for i in range(nch):
                def f(i=i):
                    c = nch - 1 - i
                    lc = sq["lc0"] + c
                    cp("act", R["Sst"][:, lc], R["Sm"][:, 1], [R["b_Sm"][1]], [R["b_Sst"][lc]])
                    state_update(1, lc, i == nch - 1, sq)
                out.append(f)
            return out

        def fwd_items(sq):
            items = []
            nch = sq["nch"]
            for c in range(nch):
                st = {}
                lc = sq["lc0"] + c
                ltok = lc * 128
                lt = lc // 4
                gtok = sq["tok0"] + c * 128
                gt = gtok // 512

                def b0(c=c, st=st, lc=lc):
                    M = [(R["Sm"][:, 0], R["b_Sm"][0]), (R["Sm2"], R["b_Sm2"])]
                    if c == 0:
                        state_init(0, sq)()
                    cur = c % 2
                    r = R["Sfr_ring"].next(); st["sf"] = r
                    cp("act", R["Sfr"][:, r], M[cur][0], [M[cur][1]], [R["b_Sfr"][r]])
                    state_update(0, lc, c == nch - 1, sq, src=M[cur], dst=M[1 - cur])

                def b1(st=st, lc=lc, ltok=ltok, lt=lt):
                    bk = bank_ring.next()
                    for dkc in range(2):
                        mm(ps[:, bk, 0:128], R["kT"][:, dkc, ltok:ltok + 128], R["qfb"][:, 0, dkc, ltok:ltok + 128], dkc == 0, dkc == 1,
                           [R["b_kT"][dkc][lt], R["b_qfb"][dkc][lt]], [b_ps[bk]], inc=(dkc == 1))
                    si = R["sc_ring"].next(); st["si"] = si
                    tt(R["sc"][:, si, :], ps[:, bk, 0:128], maskt[:, h, :], ALU.mult, [b_ps[bk], b_ret], [R["b_sc"][si]])

                def b2(st=st, lc=lc, ltok=ltok, lt=lt):
                    si = st["si"]; sf = st["sf"]
                    bk = R["o_ring"].next()
                    mm(ps[:, bk, 0:256], R["sc"][:, si, :], R["v"][:, lc, :], True, False, [R["b_sc"][si], R["b_v"][lc]], [b_ps[bk]], inc=False)
                    for dkc in range(2):
                        mm(ps[:, bk, 0:256], R["qfb"][:, 0, dkc, ltok:ltok + 128], R["Sfr"][:, sf, dkc, :], False, False,
                           [R["b_qfb"][dkc][lt], R["b_Sfr"][sf]], [b_ps[bk]], inc=False)
                    for dkc in range(2):
                        mm(ps[:, bk, 0:256], R["qfb"][:, 1, dkc, ltok:ltok + 128], R["Sst"][:, lc, dkc, :], False, dkc == 1,
                           [R["b_qfb"][dkc][lt], R["b_Sst"][lc]], [b_ps[bk]], inc=(dkc == 1))
                    su = small_ring.next()
                    st["su"] = su; st["obk"] = bk
                    sm = smallt[:, su, :]
                    S.op("dve", lambda: nc.vector.bn_stats(sm[:, 0:6], ps[:, bk, 0:256]), [b_ps[bk]], [b_small[su]])
                    S.op("dve", lambda: nc.vector.bn_aggr(sm[:, 6:8], sm[:, 0:6]), [b_small[su]], [b_small[su]])
                    ts(sm[:, 10:11], sm[:, 6:7], -1.0, None, ALU.mult, None, [b_small[su]], [b_small[su]])

                def b2b(st=st):
                    su = st["su"]; bk = st["obk"]
                    sm = smallt[:, su, :]
                    act(sm[:, 8:9], sm[:, 7:8], AF.Ln, [b_small[su]], [b_small[su]], bias=epsb[:, 0:1], scale=1.0)
                    act(sm[:, 8:9], sm[:, 8:9], AF.Exp, [b_small[su]], [b_small[su]], scale=-0.5)
                    act(sm[:, 9:10], sm[:, 10:11], AF.Identity, [b_small[su]], [b_small[su]], scale=sm[:, 8:9])
                    oi = R["on_ring"].next(); st["oi"] = oi
                    act(R["on"][:, oi, :], ps[:, bk, 0:256], AF.Identity, [b_ps[bk], b_small[su]], [R["b_on"][oi]], bias=sm[:, 9:10], scale=sm[:, 8:9])

                def b3(st=st, ltok=ltok, lt=lt, gtok=gtok, gt=gt):
                    oi = st["oi"]
                    bk = bank_ring.next()
                    pb = ps[:, bk, :].bitcast(BF16)
                    for dvc in range(2):
                        tr(pb[:, dvc * 128:(dvc + 1) * 128], R["on"][:, oi, dvc * 128:(dvc + 1) * 128], identb[:], [R["b_on"][oi], b_id],
                           [b_ps[bk]], inc=(dvc == 1))
                    tt(R["gT"][:, :, gtok:gtok + 128], pb[:, 0:256].rearrange("p (k n) -> p k n", k=2), R["sgT"][:, :, ltok:ltok + 128], ALU.mult,
                       [b_ps[bk], R["b_sgT"][0][lt], R["b_sgT"][1][lt]], [R["b_gT"][0][gt], R["b_gT"][1][gt]])
                items.append([b0, b1, b2, b2b, b3])
            return items

        partS = dict(tiles=[0, 1], rope=True, seqs=[dict(tok0=0, nch=8, pb=None, lc0=0)])
        partP = dict(tiles=[2], rope=False, seqs=[dict(tok0=1024, nch=2, pb=0, lc0=0), dict(tok0=1280, nch=2, pb=1, lc0=2)])

        def p1_lists(part, lt_order):
            first = []; second = []
            for lt in lt_order:
                t = part["tiles"][lt]
                for dkc in range(2):
                    first.append(qk_item("k", dkc, lt, t, part["rope"]))
                for dvc in range(2):
                    first.append(vg_item("v", dvc, lt, t))
            for lt, t in enumerate(part["tiles"]):
                for dkc in range(2):
                    second.append(qk_item("q", dkc, lt, t, part["rope"]))
                for dvc in range(2):
                    second.append(vg_item("g", dvc, lt, t))
            return first, second

        def flat(ss):
            return [c for st_ in ss for c in st_]

        def zipsteps(a_, b_):
            out = []
            for i in range(max(len(a_), len(b_))):
                if i < len(a_):
                    out.extend(a_[i])
                if i < len(b_):
                    out.extend(b_[i])
            return out

        def p1b_with_bwd(part, second):
            p1b = pipeline_steps(second, [0, 1, 2])
            bw = []
            for sq in part["seqs"]:
                bw.extend([[c] for c in bwd_steps(sq)])
            return zipsteps(p1b, bw)

        firstP, secondP = p1_lists(partP, [0])
        steps.extend(flat(pipeline_steps(firstP, [0, 1, 2])))
        steps.extend(p1b_with_bwd(partP, secondP))
        fitems = []
        for sq in partP["seqs"]:
            fitems.extend(fwd_items(sq))
        fwdP = pipeline_steps(fitems, [0, 0, 1, 2, 3])
        def s_items(lt):
            t = partS["tiles"][lt]
            kv = [qk_item("k", dkc, lt, t, True) for dkc in range(2)] + [vg_item("v", dvc, lt, t) for dvc in range(2)]
            qg = []
            for i in range(2):
                qg.append(qk_item("q", i, lt, t, True)); qg.append(vg_item("g", i, lt, t))
            return kv, qg
        kv1, qg1 = s_items(1)
        kv0, qg0 = s_items(0)
        stepsA = pipeline_steps(kv1 + qg1, [0, 1, 2])
        steps.extend(zipsteps(fwdP, stepsA))
        steps.extend(flat(pipeline_steps(kv0, [0, 1, 2])))
        steps.extend(p1b_with_bwd(partS, qg0))
        steps.append(lambda: (release(blkA), release(blkB)))
        p3S = pipeline_steps(fwd_items(partS["seqs"][0]), [0, 0, 1, 2, 3])

        def wo_sv():
            return wsl[:, blkO.slot, 0:2048].rearrange("p (k n) -> p k n", k=2)

        def wo_prep():
            ensure(blkO)
            sv = wo_sv()
            for kc in range(2):
                ts(sv[:, kc, :], sv[:, kc, :], gnwt[:, 2 * h + kc:2 * h + kc + 1], None, ALU.mult, None, [blkO.buf, b_const], [blkO.buf])

        def grp(oc, t):
            sv = wo_sv()
            bk = bank_ring.next()
            for kc in range(2):
                mm(ps[:, bk, :], sv[:, kc, oc * 128:(oc + 1) * 128], R["gT"][:, kc, tsl(t)], kc == 0, kc == 1,
                   [blkO.buf, R["b_gT"][kc][t]], [b_ps[bk]], inc=(kc == 1))
            resid_evac(1, 2, 0)(oc, t, bk)

        if wo_hook is None:
            parts = [wo_prep]
            gl = [(oc, t) for oc in range(8) for t in range(3)]
            for i in range(0, 24, 2):
                parts.append(lambda i=i: (grp(*gl[i]), grp(*gl[i + 1])))
            parts.append(lambda: release(blkO))
            return steps, p3S, parts

        def wo_step():
            wo_prep()
            pending = None
            for t in range(3):
                for oc in range(8):
                    grp(oc, t)
                    if pending is not None and oc == 1:
                        wo_hook(pending)
                        pending = None
                pending = t
            wo_hook(pending)
            release(blkO)
        return steps, p3S, [wo_step]

    FIN = {}

    def fin_alloc():
        if "yst" in FIN:
            return
        o = 41984
        FIN["yst"] = arena[:, o // 2:o // 2 + 3 * 2048].bitcast(F32).rearrange("p (r n) -> p r n", r=3)
        FIN["b_y"] = [S.dma_buf(f"yst{i}") for i in range(3)]
        FIN["ring"] = Ring(range(3))

    def fin_tile(t, dump=False):
        def f():
            yst = FIN["yst"]; b_y = FIN["b_y"]
            sl = tsl(t)
            bk = bank_ring.next()
            for kc in range(8):
                u = sq_ring.next()
                if kc % 2 == 0:
                    act(sqr[:, u, :], xT[:, kc, sl], AF.Square, [b_xT[kc][t]], [b_sq[u]])
                else:
                    tt(sqr[:, u, :], xT[:, kc, sl], xT[:, kc, sl], ALU.mult, [b_xT[kc][t]], [b_sq[u]])
                mm(ps[:, bk, :], onesb[:], sqr[:, u, :], kc == 0, kc == 7, [b_sq[u], b_id], [b_ps[bk]], inc=True)
            ru = nrs_ring.next()
            act(rsr[:, ru, :], ps[:, bk, :], AF.Ln, [b_ps[bk], b_id], [b_rs[ru]], bias=epsb[:, 0:1], scale=1.0 / D)
            act(rsr[:, ru, :], rsr[:, ru, :], AF.Exp, [b_rs[ru]], [b_rs[ru]], scale=-0.5)
            for kc in range(8):
                if dump:
                    break
                stt(xT[:, kc, sl], xT[:, kc, sl], fngt[:, kc:kc + 1], rsr[:, ru, :], ALU.mult, ALU.mult, [b_xT[kc][t], b_rs[ru], b_const],
                    [b_xT[kc][t]])
            for j in range(4):
                tok = t * 512 + j * 128
                yi = FIN["ring"].next()
                for half in range(2):
                    b2 = bank_ring.next()
                    for q in range(4):
                        kc = half * 4 + q
                        tr(ps[:, b2, q * 128:(q + 1) * 128], xT[:, kc, tok:tok + 128], identf[:], [b_xT[kc][t], b_id], [b_ps[b2]], inc=(q == 3))
                    evac(yst[:, yi, half * 512:(half + 1) * 512], ps[:, b2, :], [b_ps[b2]], [b_y[yi]])
                dst = o_ys[tok:tok + 128, :] if t < 2 else o_yp[tok - 1024:tok - 1024 + 128, :]
                S.dma("sp", dst, yst[:, yi, :], reads=[b_y[yi]])
        return f

    def final_phase(dump=False):
        fin_alloc()
        for t in range(3):
            fin_tile(t, dump)()

    epsb = nc.alloc_sbuf_tensor("epsb", [128, 1], F32)
    setup()
    steps = []
    marks = {}
    snaps = {}

    def snap(name):
        return lambda: snaps.__setitem__(name, S.snapshot())

    def wsnap(name):
        return lambda: S.wait_snapshot(snaps[name])
    n00 = norm_items(0, 0)
    steps.append(li_alloc)
    steps.append(setup_consts)
    steps.append(li_dma(0))
    steps.append(lambda: S.wait_tokens("pool", [LI["b"][0].w]))
    steps.append(prefetch)
    steps.append(li_dma(1))
    steps.append(setup_pt)
    steps.append(li_tr(0)); steps.append(li_tr(1)); steps.append(li_dma(2))
    steps.append(n00[0][0])
    steps.append(setup_silu)
    steps.append(mod_blk(0, 0))
    steps.append(li_tr(2))
    steps.append(snap("s0"))
    steps.append(n00[1][0])
    for hb in range(1, 4):
        steps.append(mod_blk(0, hb))
    steps.append(n00[2][0])
    steps.append(n00[0][1]); steps.append(n00[1][1]); steps.append(n00[2][1])
    steps.append(wsnap("s0"))
    steps.append(attn_alloc)
    marks['attn_alloc'] = len(steps)
    bq0 = wblk_cols(d_wqkv, 0); bq1 = wblk_cols(d_wqkv, 512); bkv = wblk_cols(d_wqkv, 1024)
    qitems = qk_items(bq0, [(i, i) for i in range(4)], False) + qk_items(bq1, [(i, 4 + i) for i in range(4)], False) \
        + qk_items(bkv, [(0, 0), (1, 1)], True)
    qitems = sorted(qitems, key=lambda it: it[5])
    steps.extend(run_pipeline([it[:5] for it in qitems], [0, 1, 2, 3, 4]))
    steps.append(attn_cache)
    steps.append(lambda: v_proj(bkv))
    steps.append(lambda: (release(bq0), release(bq1), release(bkv)))
    marks['qkv_done'] = len(steps)
    att = attention_steps()
    ex0 = [mod_blk(0, hb, spread=True) for hb in range(4, 10)]
    stride = max(1, len(att) // 8)
    for i, f in enumerate(att):
        steps.append(f)
        if ex0 and i % stride == stride - 1:
            steps.append(ex0.pop(0))
    steps.extend(ex0)
    steps.append(drain)
    marks['attn_done'] = len(steps)
    steps.append(snap("s1"))
    hk01, tail01 = norm_hooks(0, 1)
    for half in range(2):
        blk = wblk_cols(d_wo, half * 512)
        steps.append(lambda blk=blk, half=half: proj_ws(blk, 4, 8, v8, lambda kc, t: hT[:, kc, tsl(t)], lambda kc, t: [b_hT[kc][t]],
                                                        resid_evac(0, 2, half * 4), after_tile=(hk01 if half == 1 else None)))
    marks['l0_mixer_done'] = len(steps)
    steps.append(tail01)
    steps.append(wsnap("s1"))
    steps.append(mlp_alloc)
    rp = setup_ret_pieces()
    extras0 = [(0, 10), (0, 11)] + [[(1, i), rp[i]] for i in range(6)]
    hk10, tail10 = norm_hooks(1, 0, order=(2, 0, 1))
    steps.extend(mlp_steps(0, extras0, last_hook=hk10, last_order=(2, 0, 1)))
    marks['l0_done'] = len(steps)
    steps.append(snap("s2"))
    steps.append(tail10)
    steps.append(wsnap("s2"))
    steps.append(ret_alloc)
    hk11, tail11 = norm_hooks(1, 1)
    pend = []
    carry = []
    for h in range(4):
        mb = mod_blk(1, 6 + h)
        body, p3S, wparts = ret_head_steps(h, wo_hook=(hk11 if h == 3 else None))
        hs = [mb] + body
        i = 0
        for st_ in carry:
            steps.extend(st_)
            if i < len(hs):
                steps.append(hs[i]); i += 1
        for st_ in hs[i:]:
            steps.append(st_)
            if pend:
                steps.append(pend.pop(0))
        steps.extend(pend)
        steps.extend([c for st_ in p3S[:7] for c in st_])
        carry = p3S[7:]
        pend = wparts
        marks[f'ret_h{h}'] = len(steps)
    steps.extend([c for st_ in carry for c in st_])
    steps.extend(pend)
    marks['l1_mixer_done'] = len(steps)
    steps.append(snap("s3"))
    steps.append(tail11)
    steps.append(wsnap("s3"))
    steps.append(mlp_alloc)
    def fin_hook(t):
        if t == 1:
            fin_tile(0)()
        elif t == 2:
            fin_tile(1)()
    steps.append(fin_alloc)
    steps.extend(mlp_steps(1, [(1, 10), (1, 11)], last_hook=fin_hook))
    marks['l1_done'] = len(steps)
    steps.append(fin_tile(2))
    if dbg:
        print('marks', marks)
    if nsteps is not None:
        if isinstance(nsteps, str):
            nsteps = marks[nsteps]
        steps = steps[:nsteps] + [lambda: S.full_barrier(), lambda: final_phase(dump=True)]
    print('nsteps', len(steps)) if dbg else None
    for f in steps:
        f()
    S.full_barrier()
    return nc


_CACHE = {}


def _rope_tables():
    n = np.arange(1024)
    rows = (n // 64).astype(np.float64)
    cols = (n % 64).astype(np.float64)
    f32_ = 10000.0 ** (-np.arange(32, dtype=np.float32) / 32).astype(np.float32)
    ang = np.zeros((128, 1024), np.float32)
    for p in range(128):
        fr = f32_[p % 32]
        ang[p] = (rows if p < 64 else cols).astype(np.float32) * fr
    cosA = np.cos(ang).astype(np.float32)
    sinA = np.sin(ang).astype(np.float32)
    P = np.zeros((128, 128), np.float32)
    for m in range(128):
        if m % 64 < 32:
            P[m, m + 32] = -1.0
        else:
            P[m, m - 32] = 1.0
    PT = np.ascontiguousarray(P.T)
    f64_ = 10000.0 ** (-np.arange(64, dtype=np.float32) / 64).astype(np.float32)
    rc = np.zeros((128, 8, 2, 64), np.float32)
    rs = np.zeros((128, 8, 2, 64), np.float32)
    for tc in range(8):
        for p in range(128):
            tok = tc * 128 + p
            a0 = np.float32(tok // 64) * f64_
            a1 = np.float32(tok % 64) * f64_
            rc[p, tc, 0] = np.cos(a0); rc[p, tc, 1] = np.cos(a1)
            rs[p, tc, 0] = np.sin(a0); rs[p, tc, 1] = np.sin(a1)
    rtab = np.zeros((128, 160), np.float32)
    P2 = np.zeros((128, 128), np.float32)
    for p in range(128):
        fr = f64_[p % 64]
        rtab[p, 0:16] = np.cos(np.arange(16, dtype=np.float32) * fr); rtab[p, 16:32] = np.sin(np.arange(16, dtype=np.float32) * fr)
        rtab[p, 32:96] = np.cos(np.arange(64, dtype=np.float32) * fr); rtab[p, 96:160] = np.sin(np.arange(64, dtype=np.float32) * fr)
        if p < 64:
            P2[p, p + 64] = -1.0
        else:
            P2[p, p - 64] = 1.0
    rconst = np.zeros((128, 641), np.float32)
    jj = np.arange(128, dtype=np.float32)[:, None]; ii = np.arange(128, dtype=np.float32)[None, :]
    rconst[:, 0:128] = ii - jj
    rconst[:, 128:256] = (ii >= jj).astype(np.float32)
    rconst[:, 256:384] = (jj > ii).astype(np.float32)
    rconst[:, 384:512] = ii + 1.0
    rconst[:, 512:640] = 128.0 - ii
    rconst[:, 640] = np.arange(128, dtype=np.float32)
    return cosA, sinA, PT, rtab, np.ascontiguousarray(P2.T), rconst


def _pl(v):
    v = np.asarray(v, np.float32)
    lead = v.shape[:-1]
    r = v.reshape(lead + (8, 128))
    r = np.moveaxis(r, -1, 0)
    return np.ascontiguousarray(r)


def kernel(x_prompt, x_sample, cache_k, cache_v, state_ret, c, c_ctx, w_mod, b_mod, norm_g, attn_w_qkv, attn_q_norm,
           attn_k_norm, attn_w_o, ret_w_qkvg, ret_decay_logit, ret_gn_w, ret_w_o, mlp_w1, mlp_w2, final_norm_g):
    f = lambda a: np.ascontiguousarray(np.asarray(a, dtype=np.float32))
    if "nc" not in _CACHE:
        _CACHE["nc"] = build()
        _CACHE["tabs"] = _rope_tables()
    nc = _CACHE["nc"]
    cosA, sinA, PT, rtab_h, PT2, rconst = _CACHE["tabs"]
    x_prompt = f(x_prompt); x_sample = f(x_sample); cache_k = f(cache_k); cache_v = f(cache_v); state_ret = f(state_ret)
    c = f(c); c_ctx = f(c_ctx)
    shared = {
        "w_mod": f(w_mod),
        "bmod": np.ascontiguousarray(f(b_mod).reshape(2, 48, 128).transpose(2, 0, 1).reshape(128, 96)),
        "ng": np.ascontiguousarray(_pl(norm_g).reshape(128, 32)),
        "fng": np.ascontiguousarray(_pl(final_norm_g).reshape(128, 8)),
        "wqkv": f(attn_w_qkv)[0],
        "qkn": np.ascontiguousarray(np.stack([f(attn_q_norm)[0], f(attn_k_norm)[0]], axis=1)),
        "wo": f(attn_w_o)[0],
        "wr": f(ret_w_qkvg)[0],
        "dlog": np.ascontiguousarray(np.broadcast_to(f(ret_decay_logit)[0].reshape(1, 8), (128, 8))),
        "gnw": np.ascontiguousarray(_pl(f(ret_gn_w)[0]).reshape(128, 8)),
        "wro": f(ret_w_o)[0],
        "w1": f(mlp_w1), "w2": f(mlp_w2),
        "cosA": cosA, "sinA": sinA, "PTm": PT, "rtab": rtab_h, "PT2m": PT2, "rconst": rconst,
    }
    in_maps = []
    for i in range(NCORES):
        cc = np.stack([c_ctx, c[i]], axis=0)
        cT = np.ascontiguousarray(cc.reshape(2, 8, 128).transpose(2, 1, 0).reshape(128, 16))
        m = dict(shared)
        m.update({
            "xs": x_sample[i], "xp": np.ascontiguousarray(x_prompt[2 * i:2 * i + 2].reshape(512, D)),
            "ck": np.ascontiguousarray(cache_k[i, 0].reshape(512, 256)), "cv": np.ascontiguousarray(cache_v[i, 0].reshape(512, 256)),
            "sr": np.ascontiguousarray(state_ret[i, 0]), "cT": cT,
        })
        in_maps.append(m)
    res = run_bass_kernel_spmd(nc, in_maps, core_ids=list(range(NCORES)))
    rr = res.results
    y_prompt = np.concatenate([r["yp"].reshape(2, 256, D) for r in rr], axis=0)
    y_sample = np.stack([r["ys"] for r in rr], axis=0)
    nk = np.concatenate([r["nk"].reshape(2, 1, 256, 2, 128) for r in rr], axis=0)
    nv = np.concatenate([r["nv"].reshape(2, 1, 256, 2, 128) for r in rr], axis=0)
    ns = np.concatenate([r["ns"].reshape(2, 1, 2, 4, 256, 256) for r in rr], axis=0)
    return (y_prompt.astype(np.float32), y_sample.astype(np.float32), nk.astype(np.float32), nv.astype(np.float32), ns.astype(np.float32))
```
